# Optimizing a Trainium2 kernel written in Bass

```python
import math
import jax, jax.numpy as jnp
from jax import lax
import numpy as np

D_MODEL = 1024
BATCH = 8
SEQ = 4096
DEPTH = 4

MLA_HEADS = 8
MLA_Q_RANK = 256
MLA_KV_RANK = 128
MLA_NOPE = 64
MLA_ROPE = 32
MLA_V = 64
ROPE_BASE = 10000.0
DIL_PATTERNS = ((128, 1), (512, 4), (2048, 16))
DIL_GROUPS = 3
DIL_HEADS = 4
DIL_QK = 64
DIL_V = 128
DIFF_HEADS = 4
DIFF_QK = 64
DIFF_V = 2 * DIFF_QK
REL_BUCKETS = 32
REL_MAX_DIST = 128
DIL_BIAS_COLS = DIL_GROUPS * DIL_HEADS
N_BIAS = DIL_BIAS_COLS + 2 * DIFF_HEADS
N_BRANCH = 3
BRANCH_W = 512
D_FF = 4 * D_MODEL
Q_BLOCK = 128
EPS = 1e-6
NEG = -1e30

COL_SIZES = (
    MLA_Q_RANK, MLA_KV_RANK, MLA_ROPE,
    DIL_GROUPS * DIL_HEADS * DIL_QK, DIL_GROUPS * DIL_HEADS * DIL_QK, DIL_HEADS * DIL_V,
    2 * DIFF_HEADS * DIFF_QK, 2 * DIFF_HEADS * DIFF_QK, DIFF_HEADS * DIFF_V,
    N_BRANCH * D_MODEL,
)
IN_COLS = sum(COL_SIZES)
COL_SPLITS = tuple(int(v) for v in np.cumsum(COL_SIZES)[:-1])

kernel_name = 'hybrid_mla_dilated_diff_gated_encoder'

f32 = jnp.float32


def _rmsnorm(x, g):
    x32 = x.astype(f32)
    y = x32 * lax.rsqrt(jnp.mean(x32 * x32, axis=-1, keepdims=True) + EPS)
    return (y * g.astype(f32)).astype(x.dtype)


def _rope(x, pos):
    half = x.shape[-1] // 2
    inv = ROPE_BASE ** (-jnp.arange(half, dtype=f32) / half)
    ang = pos.astype(f32)[:, :, None] * inv
    cos = jnp.cos(ang)[:, :, None, :]
    sin = jnp.sin(ang)[:, :, None, :]
    x32 = x.astype(f32)
    x1, x2 = x32[..., :half], x32[..., half:]
    return jnp.concatenate([x1 * cos - x2 * sin, x2 * cos + x1 * sin], axis=-1).astype(x.dtype)


def _rel_bucket(rel):
    nb = REL_BUCKETS // 2
    max_exact = nb // 2
    ret = jnp.where(rel > 0, nb, 0)
    n = jnp.abs(rel)
    large = max_exact + (jnp.log(jnp.maximum(n, 1).astype(f32) / max_exact)
                         / math.log(REL_MAX_DIST / max_exact) * (nb - max_exact)).astype(jnp.int32)
    large = jnp.minimum(large, nb - 1)
    return ret + jnp.where(n < max_exact, n, large)


def _to_blocks(x):
    b, s = x.shape[:2]
    x = x.reshape((b, s // Q_BLOCK, Q_BLOCK) + x.shape[2:])
    return jnp.moveaxis(x, 1, 0)


def _from_blocks(y):
    y = jnp.moveaxis(y, 0, 1)
    return y.reshape((y.shape[0], y.shape[1] * y.shape[2]) + y.shape[3:])


def _mla(c_q, c_kv, k_r, pos, w_uq, g_q, w_ukv, g_kv):
    q = jnp.einsum('bsr,rhd->bshd', _rmsnorm(c_q, g_q), w_uq)
    kv = jnp.einsum('bsr,rhd->bshd', _rmsnorm(c_kv, g_kv), w_ukv)
    q = jnp.concatenate([q[..., :MLA_NOPE], _rope(q[..., MLA_NOPE:], pos)], axis=-1)
    k_rope = _rope(k_r[:, :, None, :], pos)
    k = jnp.concatenate([kv[..., :MLA_NOPE],
                         jnp.broadcast_to(k_rope, kv.shape[:3] + (MLA_ROPE,))], axis=-1)
    v = kv[..., MLA_NOPE:]
    scale = (MLA_NOPE + MLA_ROPE) ** -0.5

    def block(qb):
        s = jnp.einsum('bqhd,bkhd->bhqk', qb, k).astype(f32) * scale
        p = jax.nn.softmax(s, axis=-1).astype(v.dtype)
        return jnp.einsum('bhqk,bkhd->bqhd', p, v)

    o = _from_blocks(lax.map(block, _to_blocks(q)))
    return o.reshape(o.shape[:2] + (MLA_HEADS * MLA_V,))


def _dilated_group(q, k, v, tab_g, window, dil):
    b, n_seq, h, _ = q.shape
    half = window // (2 * dil)
    bs = half
    L = n_seq // dil
    nblk = -(-L // bs)
    Lp = nblk * bs

    def strided(x, lo, hi):
        x = jnp.swapaxes(x.reshape((b, L, dil) + x.shape[2:]), 1, 2)
        return jnp.pad(x, ((0, 0), (0, 0), (lo, hi), (0, 0), (0, 0)))

    def bands(x):
        xs = strided(x, bs, Lp - L + bs).reshape((b, dil, nblk + 2, bs) + x.shape[2:])
        return jnp.concatenate([xs[:, :, :-2], xs[:, :, 1:-1], xs[:, :, 2:]], axis=3)

    qs = strided(q, 0, Lp - L).reshape((b, dil, nblk, bs) + q.shape[2:])
    kb = bands(k)
    vb = bands(v)
    qi = jnp.arange(bs)[:, None]
    kj = jnp.arange(3 * bs)[None, :] - bs
    rel = kj - qi
    kidx = jnp.arange(nblk)[:, None, None] * bs + kj[None]
    mask = (jnp.abs(rel) <= half)[None] & (kidx >= 0) & (kidx < L)
    bias = jnp.transpose(tab_g.astype(f32)[_rel_bucket(rel * dil)], (2, 0, 1))
    s = jnp.einsum('brnqhd,brnkhd->brnhqk', qs, kb).astype(f32) * (DIL_QK ** -0.5) + bias
    s = jnp.where(mask[None, None, :, None], s, NEG)
    lse = jax.nn.logsumexp(s, axis=-1)
    p = jnp.exp(s - lse[..., None]).astype(v.dtype)
    o = jnp.einsum('brnhqk,brnkhd->brnqhd', p, vb)

    def unstrided(y):
        y = y.reshape((b, dil, Lp) + y.shape[4:])[:, :, :L]
        return jnp.swapaxes(y, 1, 2).reshape((b, n_seq) + y.shape[3:])

    return unstrided(o), unstrided(jnp.swapaxes(lse, 3, 4)[..., None])[..., 0]


def _dilated(q, k, v, bias_tab):
    outs, lses = [], []
    for g in range(DIL_GROUPS):
        window, dil = DIL_PATTERNS[g]
        o, l = _dilated_group(q[:, :, g], k[:, :, g], v,
                              bias_tab[:, g * DIL_HEADS:(g + 1) * DIL_HEADS], window, dil)
        outs.append(o)
        lses.append(l)
    alpha = jax.nn.softmax(jnp.stack(lses, axis=0), axis=0)
    o = jnp.einsum('gbsh,gbshd->bshd', alpha, jnp.stack(outs, axis=0).astype(f32)).astype(v.dtype)
    return o.reshape(o.shape[:2] + (DIL_HEADS * DIL_V,))


def _diff(q, k, v, pos, bias_tab, lq1, lk1, lq2, lk2, g_sub, layer):
    lam_init = 0.8 - 0.6 * math.exp(-0.3 * layer)
    lam = (jnp.exp(jnp.sum(lq1.astype(f32) * lk1.astype(f32)))
           - jnp.exp(jnp.sum(lq2.astype(f32) * lk2.astype(f32))) + lam_init)
    tab = bias_tab[:, DIL_BIAS_COLS:].astype(f32).reshape(REL_BUCKETS, 2, DIFF_HEADS)
    scale = DIFF_QK ** -0.5

    def block(args):
        qb, pb = args
        s = jnp.einsum('bqmhd,bkmhd->bmhqk', qb, k).astype(f32) * scale
        rel = pos[:, None, :] - pb[:, :, None]
        s = s + jnp.transpose(tab[_rel_bucket(rel)], (0, 3, 4, 1, 2))
        p = jax.nn.softmax(s, axis=-1)
        w = (p[:, 0] - lam * p[:, 1]).astype(v.dtype)
        return jnp.einsum('bhqk,bkhd->bqhd', w, v)

    o = _from_blocks(lax.map(block, (_to_blocks(q), _to_blocks(pos))))
    o = (_rmsnorm(o, g_sub).astype(f32) * (1.0 - lam_init)).astype(v.dtype)
    return o.reshape(o.shape[:2] + (DIFF_HEADS * DIFF_V,))


def setup_inputs(seed: int = 0) -> dict:
    key = jax.random.key(seed)
    ks = jax.random.split(key, 24)

    def nrm(k, shape, scale):
        return jax.random.normal(k, shape, dtype=f32) * scale

    def gain(k, shape):
        return 1.0 + 0.05 * jax.random.normal(k, shape, dtype=f32)

    positions = jnp.broadcast_to(jnp.arange(SEQ, dtype=jnp.int32)[None, :], (BATCH, SEQ))
    return {
        'x': nrm(ks[0], (BATCH, SEQ, D_MODEL), 1.0),
        'positions': positions,
        'rel_bias': nrm(ks[1], (REL_BUCKETS, N_BIAS), 0.3),
        'g_mix_pre': gain(ks[2], (DEPTH, D_MODEL)),
        'w_in': nrm(ks[3], (DEPTH, D_MODEL, IN_COLS), D_MODEL ** -0.5),
        'g_q': gain(ks[4], (DEPTH, MLA_Q_RANK)),
        'w_uq': nrm(ks[5], (DEPTH, MLA_Q_RANK, MLA_HEADS, MLA_NOPE + MLA_ROPE), MLA_Q_RANK ** -0.5),
        'g_kv': gain(ks[6], (DEPTH, MLA_KV_RANK)),
        'w_ukv': nrm(ks[7], (DEPTH, MLA_KV_RANK, MLA_HEADS, MLA_NOPE + MLA_V), MLA_KV_RANK ** -0.5),
        'lam_q1': nrm(ks[8], (DEPTH, DIFF_QK), 0.1),
        'lam_k1': nrm(ks[9], (DEPTH, DIFF_QK), 0.1),
        'lam_q2': nrm(ks[10], (DEPTH, DIFF_QK), 0.1),
        'lam_k2': nrm(ks[11], (DEPTH, DIFF_QK), 0.1),
        'g_diff_sub': gain(ks[12], (DEPTH, DIFF_V)),
        'w_branch': nrm(ks[13], (DEPTH, N_BRANCH, BRANCH_W, D_MODEL), BRANCH_W ** -0.5),
        'w_out': nrm(ks[14], (DEPTH, D_MODEL, D_MODEL), D_MODEL ** -0.5),
        'g_mix_post': gain(ks[15], (DEPTH, D_MODEL)),
        'g_mlp_pre': gain(ks[16], (DEPTH, D_MODEL)),
        'w_up': nrm(ks[17], (DEPTH, D_MODEL, D_FF), D_MODEL ** -0.5),
        'w_down': nrm(ks[18], (DEPTH, D_FF, D_MODEL), D_FF ** -0.5),
        'g_mlp_post': gain(ks[19], (DEPTH, D_MODEL)),
    }


def reference(x, positions, rel_bias, g_mix_pre, w_in, g_q, w_uq, g_kv, w_ukv,
              lam_q1, lam_k1, lam_q2, lam_k2, g_diff_sub, w_branch, w_out, g_mix_post,
              g_mlp_pre, w_up, w_down, g_mlp_post):
    b, s, _ = x.shape
    for l in range(DEPTH):
        h = _rmsnorm(x, g_mix_pre[l])
        z = h @ w_in[l]
        (c_q, c_kv, k_r, dq, dk, dv, fq, fk, fv, gz) = jnp.split(z, COL_SPLITS, axis=-1)
        o_a = _mla(c_q, c_kv, k_r, positions, w_uq[l], g_q[l], w_ukv[l], g_kv[l])
        o_b = _dilated(dq.reshape(b, s, DIL_GROUPS, DIL_HEADS, DIL_QK),
                       dk.reshape(b, s, DIL_GROUPS, DIL_HEADS, DIL_QK),
                       dv.reshape(b, s, DIL_HEADS, DIL_V), rel_bias)
        o_c = _diff(fq.reshape(b, s, 2, DIFF_HEADS, DIFF_QK),
                    fk.reshape(b, s, 2, DIFF_HEADS, DIFF_QK),
                    fv.reshape(b, s, DIFF_HEADS, DIFF_V), positions, rel_bias,
                    lam_q1[l], lam_k1[l], lam_q2[l], lam_k2[l], g_diff_sub[l], l)
        gates = jax.nn.sigmoid(gz.astype(f32)).astype(x.dtype).reshape(b, s, N_BRANCH, D_MODEL)
        merged = (gates[:, :, 0] * (o_a @ w_branch[l, 0])
                  + gates[:, :, 1] * (o_b @ w_branch[l, 1])
                  + gates[:, :, 2] * (o_c @ w_branch[l, 2]))
        x = x + _rmsnorm(merged @ w_out[l], g_mix_post[l])
        h2 = _rmsnorm(x, g_mlp_pre[l])
        f = jnp.square(jax.nn.relu(h2 @ w_up[l])) @ w_down[l]
        x = x + _rmsnorm(f, g_mlp_post[l])
    return x
```

```python
import math
from contextlib import ExitStack

import numpy as np
import ml_dtypes

import concourse.bass as bass
import concourse.mybir as mybir
from concourse.bass_utils import run_bass_kernel_spmd

F32 = mybir.dt.float32
BF16 = mybir.dt.bfloat16
I32 = mybir.dt.int32
AF = mybir.ActivationFunctionType
ALU = mybir.AluOpType
AX = mybir.AxisListType

S = 4096
D = 1024
DEPTH = 4
NCORES = 8
IN_COLS = 7072
EPS = 1e-6
NT = S // 128
NTT = S // 512

C_CQ, C_CKV, C_KR, C_DQ, C_DK, C_DV, C_FQ, C_FK, C_FV, C_GZ = (
    0, 256, 384, 416, 1184, 1952, 2464, 2976, 3488, 4000)

ENG_ATTR = {'pe': 'tensor', 'act': 'scalar', 'dve': 'vector', 'pool': 'gpsimd', 'sp': 'sync'}
NDMA_SEMS = 6
VARIANT = ""
ROPE_ENG = "dve"
LAT_NT = None


class _Op:
    __slots__ = ('eng', 'fn', 'dma', 'deps', 'needed', 'tok', 'raw')

    def __init__(self, eng, fn, dma):
        self.eng = eng
        self.fn = fn
        self.dma = dma
        self.deps = []
        self.needed = False
        self.tok = None


class Phase:
    def __init__(self, nc, name):
        self.nc = nc
        self.name = name
        self.ops = {e: [] for e in ENG_ATTR}
        self.all_ops = []
        self.last_writer = {}
        self.readers = {}

    def op(self, eng, fn, r=(), w=(), dma=False, x=()):
        o = _Op(eng, fn, dma)
        deps = {}
        raw_keys = set(r) | set(x)
        r = list(r)
        w = list(w) + list(x)
        for k in raw_keys:
            d = self.last_writer.get(k)
            if d is not None:
                deps[id(d)] = (d, True)
        for k in w:
            d = self.last_writer.get(k)
            if d is not None and id(d) not in deps:
                deps[id(d)] = (d, False)
            for d in self.readers.get(k, ()):
                if id(d) not in deps:
                    deps[id(d)] = (d, False)
        for d, raw in deps.values():
            if d is o:
                continue
            if d.eng == eng and not d.dma:
                if eng == 'pe' or not raw:
                    continue
            o.deps.append(d)
        for k in w:
            self.last_writer[k] = o
            self.readers[k] = []
        for k in r:
            lst = self.readers.setdefault(k, [])
            if not dma:
                lst[:] = [d for d in lst if d.dma or d.eng != eng]
            lst.append(o)
        self.ops[eng].append(o)
        self.all_ops.append(o)
        return o

    def dma(self, q, out, in_, r=(), w=(), **kw):
        return self.op(q, lambda e: e.dma_start(out=out, in_=in_, **kw), r=r, w=w, dma=True)

    def emit(self):
        nc = self.nc
        allsems = []

        def newsem(nm):
            h = nc.alloc_semaphore(name=nm)
            allsems.append(h)
            return h

        dma_sems = {}
        dma_cnt = {}
        last_user = {}
        nd = {}
        for o in self.all_ops:
            if o.dma:
                q = o.eng
                if q not in dma_sems:
                    dma_sems[q] = [newsem(f"{self.name}_{q}_d{i}") for i in range(NDMA_SEMS)]
                    nd[q] = 0
                i = nd[q] % NDMA_SEMS
                nd[q] += 1
                key = (q, i)
                prev = last_user.get(key)
                if prev is not None:
                    o.deps.append(prev)
                last_user[key] = o
                dma_cnt[key] = dma_cnt.get(key, 0) + 16
                o.tok = (dma_sems[q][i], dma_cnt[key])
                o.needed = True
        for o in self.all_ops:
            for d in o.deps:
                d.needed = True
        eng_sem = {}
        eng_cnt = {}
        for o in self.all_ops:
            if o.dma or not o.needed:
                continue
            e = o.eng
            if e not in eng_sem:
                eng_sem[e] = newsem(f"{self.name}_{e}_c")
                eng_cnt[e] = 0
            eng_cnt[e] += 1
            o.tok = (eng_sem[e], eng_cnt[e])

        def make(e):
            ops = self.ops[e]

            def body(engobj):
                waited = {}
                for o in ops:
                    need = {}
                    for d in o.deps:
                        sem, val = d.tok
                        if need.get(id(sem), (None, 0))[1] < val:
                            need[id(sem)] = (sem, val)
                    for sem, val in need.values():
                        if waited.get(id(sem), 0) < val:
                            engobj.wait_ge(sem, val)
                            waited[id(sem)] = val
                    ins = o.fn(engobj)
                    if o.dma:
                        ins.then_inc(o.tok[0], 16)
                    elif o.needed:
                        ins.then_inc(o.tok[0], 1)
                for (q, i), cnt in dma_cnt.items():
                    if q == e:
                        sem = dma_sems[q][i]
                        if waited.get(id(sem), 0) < cnt:
                            engobj.wait_ge(sem, cnt)
            return body

        with nc.Block() as block:
            for e, attr in ENG_ATTR.items():
                if self.ops[e]:
                    getattr(block, attr)(make(e))
        nc.clear_and_free_semaphores(allsems)
        nc.all_engine_barrier()


def bcast_row(ap_row, nparts=128):
    t = ap_row.tensor
    n = ap_row.ap[-1][1]
    return bass.AP(t, ap_row.offset, [[0, nparts], [1, n]])


def phase_norm_T(nc, name, x_src, g_row, hT, ident, ph=None, st=None, after_chunk=None):
    own = ph is None
    if own:
        ph = Phase(nc, name)
        st = ExitStack()
    if True:
        sb = lambda n, shp, dt: st.enter_context(nc.sbuf_tensor(f"{name}_{n}", shp, dt))
        xt = [sb(f"xt{i}", [128, D], F32) for i in range(2)]
        gbc = sb("gbc", [128, D], F32)
        junk = [sb(f"junk{i}", [128, D], BF16) for i in range(2)]
        ss = [sb(f"ss{i}", [128, 1], F32) for i in range(2)]
        rs = [sb(f"rs{i}", [128, 1], F32) for i in range(2)]
        hb = [sb(f"hb{i}", [128, D], BF16) for i in range(2)]
        psT = [st.enter_context(nc.psum_tensor(f"{name}_psT{i}", [128, 8, 128], BF16)) for i in range(2)]
        epst = sb("eps", [128, 1], F32)
        ph.op('pool', lambda e: e.memset(epst[:], EPS), w=['eps'])
        ph.dma('sp', gbc[:], bcast_row(g_row), w=['gbc'])
        for tc in range(NT):
            s = tc % 2
            ph.dma('sp', xt[s][:], x_src[tc * 128:(tc + 1) * 128, :], w=[('xt', s)])
            ph.op('act', lambda e, s=s: e.activation(out=junk[s][:], in_=xt[s][:], func=AF.Square,
                                                     accum_out=ss[s][:]),
                  r=[('xt', s)], w=[('ss', s), ('junk', s)])
            ph.op('act', lambda e, s=s: e.activation(out=rs[s][:], in_=ss[s][:], func=AF.Sqrt,
                                                     scale=1.0 / D, bias=epst[:]),
                  r=[('ss', s), 'eps'], w=[('rs', s)])
            ph.op('dve', lambda e, s=s: e.reciprocal(out=rs[s][:], in_=rs[s][:]),
                  r=[('rs', s)], w=[('rs', s)])
            ph.op('dve', lambda e, s=s: e.scalar_tensor_tensor(out=hb[s][:], in0=xt[s][:],
                                                               scalar=rs[s][:, 0:1], in1=gbc[:],
                                                               op0=ALU.mult, op1=ALU.mult),
                  r=[('xt', s), ('rs', s), 'gbc'], w=[('hb', s)])
            for k in range(8):
                ph.op('pe', lambda e, s=s, k=k: e.transpose(psT[s][:, k, :], hb[s][:, k * 128:(k + 1) * 128],
                                                            ident[:]),
                      r=[('hb', s)], w=[('psT', s)])
            if tc % 2 == 0:
                ph.op('act', lambda e, s=s, tc=tc: e.copy(out=hT[:, :, tc * 128:(tc + 1) * 128], in_=psT[s][:]),
                      r=[('psT', s)], w=[('hT', tc)])
            else:
                ph.op('dve', lambda e, s=s, tc=tc: e.tensor_copy(out=hT[:, :, tc * 128:(tc + 1) * 128],
                                                                 in_=psT[s][:]),
                      r=[('psT', s)], w=[('hT', tc)])
            if after_chunk is not None:
                after_chunk(tc)
        if own:
            ph.emit()
            st.close()


ZT_ROWS = 5632
ZR_DQ, ZR_DK, ZR_FQ, ZR_FK, ZR_G = 0, 768, 1536, 2048, 2560


def zt_src_col(r):
    if r < 1536:
        return C_DQ + r
    if r < 2560:
        return C_FQ + (r - 1536)
    return C_GZ + (r - 2560)


def phase_win(nc, name, w_in_l, hT, zT, lat, dvn, fvn, pre=None):
    ph = Phase(nc, name)
    with ExitStack() as st:
        sb = lambda n, shp, dt: st.enter_context(nc.sbuf_tensor(f"{name}_{n}", shp, dt))
        wt = [sb(f"wt{i}", [128, 8, 256], BF16) for i in range(2)]
        wtok = sb("wtok", [128, 8, 1440], BF16)
        stage = [sb(f"stage{i}", [128, S], BF16) for i in range(2)]
        tstage = [sb(f"tstage{i}", [128, 1024], BF16) for i in range(2)]
        lstage = [sb(f"lstage{i}", [128, 416], F32) for i in range(2)]
        ps = [st.enter_context(nc.psum_tensor(f"{name}_ps{i}", [128, 512], F32)) for i in range(6)]
        wT = w_in_l.tensor
        woff = w_in_l.offset

        def wsrc(c0, n):
            return bass.AP(wT, woff + c0, [[IN_COLS, 128], [128 * IN_COLS, 8], [1, n]])

        ph.dma('pool', wtok[:, :, 0:416], wsrc(0, 416), w=['wtok'])
        ph.dma('pool', wtok[:, :, 416:928], wsrc(C_DV, 512), w=['wtok'])
        ph.dma('pool', wtok[:, :, 928:1440], wsrc(C_FV, 512), w=['wtok'])
        pi = 0
        nblk = ZT_ROWS // 256
        tok_iter = iter(range(NT))

        def tok_chunk(tc):
            nonlocal pi
            s = tc % 2
            for gi, (c0, n) in enumerate(((0, 416), (416, 512), (928, 512))):
                b = pi % 6
                pi += 1
                for kc in range(8):
                    ph.op('pe', lambda e, b=b, kc=kc, tc=tc, c0=c0, n=n: e.matmul(
                        ps[b][:, 0:n], hT[:, kc, tc * 128:(tc + 1) * 128], wtok[:, kc, c0:c0 + n],
                        start=(kc == 0), stop=(kc == 7)),
                        r=['wtok', ('hT', tc)], w=[('ps', b)])
                if gi == 0:
                    ph.op('dve', lambda e, b=b, s=s: e.tensor_copy(out=lstage[s][:], in_=ps[b][:, 0:416]),
                          r=[('ps', b)], w=[('lstage', s)])
                    ph.dma('sp', lat[tc * 128:(tc + 1) * 128, :], lstage[s][:], r=[('lstage', s)])
                else:
                    o0 = (gi - 1) * 512
                    ph.op('dve', lambda e, b=b, s=s, o0=o0: e.tensor_copy(out=tstage[s][:, o0:o0 + 512],
                                                                          in_=ps[b][:]),
                          r=[('ps', b)], w=[('tstage', s)])
                    dst = dvn if gi == 1 else fvn
                    ph.dma('sp', dst[tc * 128:(tc + 1) * 128, :], tstage[s][:, o0:o0 + 512],
                           r=[('tstage', s)])

        if pre is not None:
            pre(ph, st, None)
        for blk in range(nblk):
            ws = blk % 2
            r0 = blk * 256
            ph.dma('pool', wt[ws][:], wsrc(zt_src_col(r0), 256), w=[('wt', ws)])
            for mi in range(2):
                row = r0 + mi * 128
                ss_ = (blk * 2 + mi) % 2
                is_gate = row >= ZR_G
                for tt in range(NTT):
                    b = pi % 6
                    pi += 1
                    for kc in range(8):
                        ph.op('pe', lambda e, b=b, kc=kc, tt=tt, ws=ws, mi=mi: e.matmul(
                            ps[b][:], wt[ws][:, kc, mi * 128:(mi + 1) * 128], hT[:, kc, tt * 512:(tt + 1) * 512],
                            start=(kc == 0), stop=(kc == 7)),
                            r=[('wt', ws)] + [('hT', tt * 4 + j) for j in range(4)], w=[('ps', b)])
                    if is_gate:
                        ph.op('act', lambda e, b=b, ss_=ss_, tt=tt: e.activation(
                            out=stage[ss_][:, tt * 512:(tt + 1) * 512], in_=ps[b][:], func=AF.Sigmoid),
                            r=[('ps', b)], w=[('stage', ss_)])
                    elif tt % 2 == 0:
                        ph.op('act', lambda e, b=b, ss_=ss_, tt=tt: e.copy(
                            out=stage[ss_][:, tt * 512:(tt + 1) * 512], in_=ps[b][:]),
                            r=[('ps', b)], w=[('stage', ss_)])
                    else:
                        ph.op('dve', lambda e, b=b, ss_=ss_, tt=tt: e.tensor_copy(
                            out=stage[ss_][:, tt * 512:(tt + 1) * 512], in_=ps[b][:]),
                            r=[('ps', b)], w=[('stage', ss_)])
                ph.dma('sp', zT[row:row + 128, :], stage[ss_][:], r=[('stage', ss_)])
            ntok = 2 if blk < 10 else 1
            for _ in range(ntok):
                tc = next(tok_iter, None)
                if tc is not None:
                    tok_chunk(tc)
        for tc in tok_iter:
            tok_chunk(tc)
        ph.emit()


def MM(ph, out, lhsT, rhs, start, stop, r, w):
    ph.op('pe', lambda e: e.matmul(out, lhsT, rhs, start=start, stop=stop, skip_group_check=True), r=r, w=w)


def TR(ph, out, in_, ident, r, w):
    ph.op('pe', lambda e: e.transpose(out, in_, ident), r=r, w=w)


def ACTF(ph, out, in_, func, r, w, x=(), **kw):
    ph.op('act', lambda e: e.activation(out=out, in_=in_, func=func, **kw), r=r, w=w, x=x)


def CP(ph, eng, out, in_, r, w, x=()):
    if eng == 'act':
        ph.op('act', lambda e: e.copy(out=out, in_=in_), r=r, w=w, x=x)
    else:
        ph.op(eng, lambda e: e.tensor_copy(out=out, in_=in_), r=r, w=w, x=x)


def TT(ph, eng, out, in0, in1, op, r, w, x=()):
    ph.op(eng, lambda e: e.tensor_tensor(out=out, in0=in0, in1=in1, op=op), r=r, w=w, x=x)


def TS(ph, eng, out, in0, s1, s2, op0, op1, r, w):
    if op1 is None:
        ph.op(eng, lambda e: e.tensor_scalar(out=out, in0=in0, scalar1=s1, scalar2=None, op0=op0), r=r, w=w)
    else:
        ph.op(eng, lambda e: e.tensor_scalar(out=out, in0=in0, scalar1=s1, scalar2=s2, op0=op0, op1=op1),
              r=r, w=w)


def STT(ph, eng, out, in0, scalar, in1, op0, op1, r, w, x=()):
    ph.op(eng, lambda e: e.scalar_tensor_tensor(out=out, in0=in0, scalar=scalar, in1=in1, op0=op0, op1=op1),
          r=r, w=w, x=x)


def RECIP(ph, out, in_, r, w, x=()):
    ph.op('dve', lambda e: e.reciprocal(out=out, in_=in_), r=r, w=w, x=x)


def MEMSET(ph, eng, ap, val, w):
    ph.op(eng, lambda e: e.memset(ap, val), w=w)


def phase_init(nc, name, ident, ident_d, pos_d, invf_d, cosT, sinT, onesb, onesf):
    ph = Phase(nc, name)
    with ExitStack() as st:
        sb = lambda n, shp, dt: st.enter_context(nc.sbuf_tensor(f"{name}_{n}", shp, dt))
        posi = sb("posi", [128, NT], I32)
        posf = sb("posf", [128, NT], F32)
        invf = sb("invf", [128, 16], F32)
        ang = sb("ang", [128, NT, 16], F32)
        a = sb("a", [128, NT, 16], F32)
        u = sb("u", [128, NT, 16], F32)
        ki = sb("ki", [128, NT, 16], I32)
        kf = sb("kf", [128, NT, 16], F32)
        r_ = sb("r", [128, NT, 16], F32)
        s2 = sb("s2", [128, NT, 16], F32)
        s4 = sb("s4", [128, NT, 16], F32)
        ph.dma('sp', ident[:], ident_d, w=['ident'])
        ph.dma('sp', posi[:], pos_d, w=['posi'])
        ph.dma('sp', invf[:], invf_d, w=['invf'])
        MEMSET(ph, 'pool', onesb[:], 1.0, ['onesb'])
        MEMSET(ph, 'pool', onesf[:], 1.0, ['onesf'])
        CP(ph, 'dve', posf[:], posi[:], ['posi'], ['posf'])
        for c in range(NT):
            TS(ph, 'dve', ang[:, c, :], invf[:], posf[:, c:c + 1], None, ALU.mult, None, ['posf', 'invf'], ['ang'])
        for tab, shift, nm in ((sinT, 0.0, 'sinT'), (cosT, math.pi / 2, 'cosT')):
            TS(ph, 'dve', a[:], ang[:], shift, None, ALU.add, None, ['ang'], ['a'])
            TS(ph, 'dve', u[:], a[:], 1.0 / (2 * math.pi), None, ALU.mult, None, ['a'], ['u'])
            CP(ph, 'dve', ki[:], u[:], ['u'], ['ki'])
            CP(ph, 'dve', kf[:], ki[:], ['ki'], ['kf'])
            STT(ph, 'dve', r_[:], kf[:], -2 * math.pi, a[:], ALU.mult, ALU.add, ['kf', 'a'], ['r'])
            ACTF(ph, s2[:], r_[:], AF.Sin, ['r'], ['s2'], scale=0.5)
            ACTF(ph, s4[:], r_[:], AF.Sin, ['r'], ['s4'], scale=0.25)
            TT(ph, 'dve', u[:], s4[:], s4[:], ALU.mult, ['s4'], ['u'])
            TS(ph, 'dve', u[:], u[:], -2.0, 1.0, ALU.mult, ALU.add, ['u'], ['u'])
            STT(ph, 'dve', tab[:], s2[:], 2.0, u[:], ALU.mult, ALU.mult, ['s2', 'u'], [nm])
        ph.emit()


def phase_latprep(nc, name, lat, gq_row, gkv_row, cosT, sinT, cqnT, ckvnT, krT, ident):
    ph = Phase(nc, name)
    with ExitStack() as st:
        sb = lambda n, shp, dt: st.enter_context(nc.sbuf_tensor(f"{name}_{n}", shp, dt))
        lt = [sb(f"lt{i}", [128, 416], F32) for i in range(2)]
        gqb = sb("gqb", [128, 256], F32)
        gkvb = sb("gkvb", [128, 128], F32)
        junk = [sb(f"junk{i}", [128, 256], BF16) for i in range(2)]
        ssa = [sb(f"ssa{i}", [128, 1], F32) for i in range(2)]
        ssb = [sb(f"ssb{i}", [128, 1], F32) for i in range(2)]
        rsa = [sb(f"rsa{i}", [128, 1], F32) for i in range(2)]
        rsb = [sb(f"rsb{i}", [128, 1], F32) for i in range(2)]
        cb = [sb(f"cb{i}", [128, 384], BF16) for i in range(2)]
        kb = [sb(f"kb{i}", [128, 96], BF16) for i in range(2)]
        tq = [[sb(f"tq{i}_{j}", [128, 16], F32) for j in range(4)] for i in range(2)]
        epst = sb("eps", [128, 1], F32)
        psT = [st.enter_context(nc.psum_tensor(f"{name}_psT{i}", [128, 8, 128], BF16)) for i in range(2)]
        MEMSET(ph, 'pool', epst[:], EPS, ['eps'])
        for i in range(2):
            MEMSET(ph, 'pool', kb[i][:], 0.0, [('kb', i)])
        ph.dma('sp', gqb[:], bcast_row(gq_row), w=['gqb'])
        ph.dma('sp', gkvb[:], bcast_row(gkv_row), w=['gkvb'])
        for tc in range(LAT_NT or NT):
            s = tc % 2
            L = lt[s]
            ph.dma('sp', L[:], lat[tc * 128:(tc + 1) * 128, :], w=[('lt', s)])
            ACTF(ph, junk[s][:, 0:256], L[:, 0:256], AF.Square, [('lt', s)], [('junk', s), ('ssa', s)],
                 accum_out=ssa[s][:])
            ACTF(ph, junk[s][:, 128:256], L[:, 256:384], AF.Square, [('lt', s)], [('junkb', s), ('ssb', s)],
                 accum_out=ssb[s][:])
            ACTF(ph, rsa[s][:], ssa[s][:], AF.Sqrt, [('ssa', s), 'eps'], [('rsa', s)],
                 scale=1.0 / 256, bias=epst[:])
            ACTF(ph, rsb[s][:], ssb[s][:], AF.Sqrt, [('ssb', s), 'eps'], [('rsb', s)],
                 scale=1.0 / 128, bias=epst[:])
            RECIP(ph, rsa[s][:], rsa[s][:], [('rsa', s)], [('rsa', s)])
            RECIP(ph, rsb[s][:], rsb[s][:], [('rsb', s)], [('rsb', s)])
            STT(ph, 'dve', cb[s][:, 0:256], L[:, 0:256], rsa[s][:, 0:1], gqb[:], ALU.mult, ALU.mult,
                [('lt', s), ('rsa', s), 'gqb'], [('cb', s)])
            STT(ph, 'dve', cb[s][:, 256:384], L[:, 256:384], rsb[s][:, 0:1], gkvb[:], ALU.mult, ALU.mult,
                [('lt', s), ('rsb', s), 'gkvb'], [('cb', s)])
            if VARIANT == "norope":
                continue
            x1 = L[:, 384:400]
            x2 = L[:, 400:416]
            c_ = cosT[:, tc, :]
            s_ = sinT[:, tc, :]
            t1, t2, t3, t4 = tq[s]
            TT(ph, ROPE_ENG, t2[:], x2, s_, ALU.mult, [('lt', s)], [('t2', s)])
            TT(ph, ROPE_ENG, t4[:], x1, s_, ALU.mult, [('lt', s)], [('t4', s)])
            TT(ph, 'dve', t1[:], x1, c_, ALU.mult, [('lt', s)], [('t1', s)])
            TT(ph, 'dve', t3[:], x2, c_, ALU.mult, [('lt', s)], [('t3', s)])
            if VARIANT != "R1":
                TT(ph, 'dve', kb[s][:, 64:80], t1[:], t2[:], ALU.subtract, [('t1', s), ('t2', s)], [('kb', s)])
            if VARIANT not in ("R1", "R2"):
                TT(ph, 'dve', kb[s][:, 80:96], t3[:], t4[:], ALU.add, [('t3', s), ('t4', s)], [('kb', s)])
            if VARIANT == "notr":
                continue
            for j in range(3):
                TR(ph, psT[s][:, j, :], cb[s][:, j * 128:(j + 1) * 128], ident[:], [('cb', s)], [('psT', s)])
            if VARIANT not in ("nokb", "noact", "nodve", "nocp"):
                TR(ph, psT[s][0:96, 3, :], kb[s][:, 0:96], ident[:], [('kb', s)], [('psT', s)])
            sl = slice(tc * 128, (tc + 1) * 128)
            if VARIANT not in ("noact", "nocp"):
                CP(ph, 'act', cqnT[:, :, sl], psT[s][:, 0:2, :], [], [('cqnT', tc)], x=[('psT', s)])
            if VARIANT not in ("nodve", "nocp"):
                CP(ph, 'dve', ckvnT[:, sl], psT[s][:, 2, :], [], [('ckvnT', tc)], x=[('psT', s)])
            if VARIANT not in ("nokb", "noact", "nodve", "nocp"):
                CP(ph, 'act', krT[64:96, sl], psT[s][64:96, 3, :], [], [('krT', tc)], x=[('psT', s)])
        ph.emit()


def PV_(t, p0, npart, off, dims):
    P = 1
    for d in t.shape[1:]:
        P *= d
    return bass.AP(t, p0 * P + off, [[P, npart]] + [list(d) for d in dims])


def phase_mla(nc, name, w_uq_l, w_ukv_l, cqnT, ckvnT, krT, cosT, sinT, oaT, ident, onesf, extra=None):
    ph = Phase(nc, name)
    with ExitStack() as st:
        sb = lambda n, shp, dt: st.enter_context(nc.sbuf_tensor(f"{name}_{n}", shp, dt))
        pst = lambda n, shp, dt: st.enter_context(nc.psum_tensor(f"{name}_{n}", shp, dt))
        wuq = sb("wuq", [128, 2, 768], BF16)
        wukv = sb("wukv", [128, 1024], BF16)
        Vall = sb("Vall", [128, NT, 8, 128], BF16)
        KT = [sb(f"KT{i}", [128, S], BF16) for i in range(2)]
        QT = [sb(f"QT{i}", [128, S], BF16) for i in range(2)]
        qb = [sb(f"qb{i}", [128, 4, 96], BF16) for i in range(2)]
        tq = [[sb(f"tq{i}_{j}", [128, 4, 16], F32) for j in range(4)] for i in range(2)]
        PT = [sb(f"PT{i}", [128, 512], BF16) for i in range(4)]
        rz = [sb(f"rz{i}", [128, 512], F32) for i in range(2)]
        bcs = [sb(f"bcs{i}", [64, 512], F32) for i in range(2)]
        ost = [sb(f"ost{i}", [64, 512], BF16) for i in range(2)]
        pS = [pst(f"pS{i}", [128, 512], F32) for i in range(4)]
        pO = [pst(f"pO{i}", [128, 512], F32) for i in range(2)]
        pG = pst("pG", [128, 512], F32)
        psQT = pst("psQT", [128, 8, 128], BF16)
        t_uq = w_uq_l.tensor
        ph.dma('pool', wuq[:], bass.AP(t_uq, w_uq_l.offset, [[768, 128], [128 * 768, 2], [1, 768]]), w=['wuq'])
        ph.dma('pool', wukv[:], bass.AP(w_ukv_l.tensor, w_ukv_l.offset, [[1024, 128], [1, 1024]]), w=['wukv'])
        MEMSET(ph, 'pool', Vall[:, :, :, 64:128], 1.0, ['Vall1'])
        if extra is not None:
            extra(ph)
        for c in range(NT):
            MM(ph, PV_(pG, 0, 128, 0, [[64, 8], [1, 64]]), ckvnT[:, c * 128:(c + 1) * 128],
               PV_(wukv, 0, 128, 64, [[128, 8], [1, 64]]), True, True, ['wukv'], ['pG'])
            CP(ph, 'dve', Vall[:, c, :, 0:64], PV_(pG, 0, 128, 0, [[64, 8], [1, 64]]), [], ['Vall'], x=['pG'])
        sc = 96 ** -0.5

        def kproj(h, s, tt):
            MM(ph, pG[0:64, :], wukv[:, h * 128:h * 128 + 64], ckvnT[:, tt * 512:(tt + 1) * 512], True, True,
               ['wukv'], ['pG'])
            CP(ph, 'dve', KT[s][0:64, tt * 512:(tt + 1) * 512], pG[0:64, :], [], [('KT', s)], x=['pG'])

        def q_tr(s, tg, b2):
            for c4 in range(4):
                TR(ph, psQT[0:96, c4, :], qb[b2][:, c4, :], ident[:], [('qb', b2)], ['psQT'])
            CP(ph, 'dve', PV_(QT[s], 0, 96, tg * 512, [[128, 4], [1, 128]]), psQT[0:96, 0:4, :], [], [('QT', s)],
               x=['psQT'])

        def qproj(h, s, tg):
            b2 = tg % 2
            gv = lambda a, b: PV_(pG, 0, 128, a, [[96, 4], [1, b]])
            for c4 in range(4):
                tc = tg * 4 + c4
                for j in range(2):
                    MM(ph, PV_(pG, 0, 128, c4 * 96, [[1, 96]]), cqnT[:, j, tc * 128:(tc + 1) * 128],
                       wuq[:, j, h * 96:(h + 1) * 96], (c4 == 0 and j == 0), (j == 1), ['wuq'], ['pG'])
            if tg > 0:
                q_tr(s, tg - 1, (tg - 1) % 2)
            x1 = gv(64, 16)
            x2 = gv(80, 16)
            c_ = cosT[:, tg * 4:(tg + 1) * 4, :]
            s_ = sinT[:, tg * 4:(tg + 1) * 4, :]
            t1, t2, t3, t4 = tq[b2]
            TT(ph, 'dve', t1[:], x1, c_, ALU.mult, [], [('t1', b2)], x=['pG'])
            TT(ph, 'dve', t2[:], x2, s_, ALU.mult, [], [('t2', b2)], x=['pG'])
            TT(ph, 'dve', t3[:], x2, c_, ALU.mult, [], [('t3', b2)], x=['pG'])
            TT(ph, 'dve', t4[:], x1, s_, ALU.mult, [], [('t4', b2)], x=['pG'])
            CP(ph, 'dve', qb[b2][:, :, 0:64], gv(0, 64), [], [('qb', b2)], x=['pG'])
            TT(ph, 'pool', qb[b2][:, :, 64:80], t1[:], t2[:], ALU.subtract, [('t1', b2), ('t2', b2)], [('qb', b2)])
            TT(ph, 'pool', qb[b2][:, :, 80:96], t3[:], t4[:], ALU.add, [('t3', b2), ('t4', b2)], [('qb', b2)])

        def make_prologue(h):
            s = h % 2
            ch = [(lambda tt=tt: kproj(h, s, tt)) for tt in range(NTT)]
            ch.append(lambda: CP(ph, 'pool', KT[s][64:96, :], krT[64:96, :], [], [('KT', s)]))
            ch += [(lambda tg=tg: qproj(h, s, tg)) for tg in range(8)]
            ch.append(lambda: q_tr(s, 7, 1))
            return ch

        for c in make_prologue(0):
            c()
        pending = []
        for h in range(8):
            s = h % 2
            nxt = make_prologue(h + 1) if h < 7 else []
            for qt in range(NTT):
                po = pO[qt % 2]
                q2 = qt % 2
                rhsQ = QT[s][0:96, qt * 512:(qt + 1) * 512]

                def pv(kc):
                    i4 = kc % 4
                    MM(ph, po[:, :], Vall[:, kc, h, :], PT[i4][:], kc == 0, kc == NT - 1,
                       [('PT', i4), 'Vall', 'Vall1'], [('pO', q2)])

                for kc in range(NT):
                    b = kc % 4
                    MM(ph, pS[b][:], KT[s][0:96, kc * 128:(kc + 1) * 128], rhsQ, True, True,
                       [('KT', s), ('QT', s)], [('pS', b)])
                    ACTF(ph, PT[b][:], pS[b][:], AF.Exp, [('pS', b)], [('PT', b)], scale=sc)
                    if kc > 1:
                        pv(kc - 2)
                    if kc == 10 and pending:
                        pending.pop(0)()
                    if kc % 8 == 4 and nxt:
                        nxt.pop(0)()
                pv(NT - 2)
                pv(NT - 1)
                RECIP(ph, rz[q2][64:65, :], po[64:65, :], [], [('rz', q2)], x=[('pO', q2)])

                def epi(h=h, qt=qt, q2=q2, po=po):
                    MM(ph, pG[0:64, :], onesf[64:65, 0:64], rz[q2][64:65, :], True, True, [('rz', q2)], ['pG'])
                    CP(ph, 'dve', bcs[q2][:], pG[0:64, :], [], [('bcs', q2)], x=['pG'])
                    TT(ph, 'dve', ost[q2][:], po[0:64, :], bcs[q2][:], ALU.mult, [('bcs', q2)], [('ost', q2)],
                       x=[('pO', q2)])
                    ph.dma('pool', oaT[h * 64:(h + 1) * 64, qt * 512:(qt + 1) * 512], ost[q2][:], r=[('ost', q2)])

                pending.append(epi)
            for c in nxt:
                c()
        for c in pending:
            c()
        ph.emit()


DIL = (1, 4, 16)


def split_dma(ph, q, out_fn, in_fn, n, step, **kw):
    for a in range(0, n, step):
        b = min(n, a + step)
        ph.dma(q, out_fn(a, b), in_fn(a, b), **kw)


def phase_bias(nc, name, dbias_d, fbias_d, farb_d, dE, fE, farb):
    ph = Phase(nc, name)
    with ExitStack() as st:
        sb = lambda n, shp, dt: st.enter_context(nc.sbuf_tensor(f"{name}_{n}", shp, dt))
        t1 = sb("t1", [128, 36, 128], F32)
        t2 = sb("t2", [128, 40, 128], F32)
        ph.dma('sp', t1[:], dbias_d, w=['t1'])
        ph.dma('sp', t2[:], fbias_d, w=['t2'])
        ph.dma('sp', farb[:], farb_d, w=['farb'])
        TS(ph, 'dve', dE[:], t1[:], 8.0, None, ALU.mult, None, ['t1'], ['dE'])
        TS(ph, 'dve', fE[:], t2[:], 8.0, None, ALU.mult, None, ['t2'], ['fE'])
        ph.emit()


def phase_dil(nc, name, zT, dvn, dE, obT, onesb, ident, extra=None):
    ph = Phase(nc, name)
    with ExitStack() as st:
        if extra is not None:
            extra(ph)
        sb = lambda n, shp, dt: st.enter_context(nc.sbuf_tensor(f"{name}_{n}", shp, dt))
        pst = lambda n, shp, dt: st.enter_context(nc.psum_tensor(f"{name}_{n}", shp, dt))
        QT = [sb(f"QT{g}", [64, S], BF16) for g in range(3)]
        KT = [sb(f"KT{g}", [64, S], BF16) for g in range(3)]
        nch = [S // d // 128 + 1 for d in DIL]
        Vg = [sb(f"Vg{g}", [128, DIL[g], nch[g], 128], BF16) for g in range(3)]
        acc = sb("acc", [128, 2, S], F32)
        PT = [sb(f"PT{i}", [128, 256], BF16) for i in range(3)]
        sbias = [sb(f"sbias{i}", [128, 256], F32) for i in range(3)]
        rz = [sb(f"rz{i}", [128, 512], F32) for i in range(2)]
        ost = [sb(f"ost{i}", [128, 512], BF16) for i in range(2)]
        pS = [pst(f"pS{i}", [128, 512], F32) for i in range(2)]
        pO = [pst(f"pO{i}", [128, 512], F32) for i in range(2)]
        zt_t = zT.tensor
        dv_t = dvn.tensor
        blk = 0
        for h in range(4):
            for g in range(3):
                d = DIL[g]
                L = S // d
                n_ = nch[g]
                rq = ZR_DQ + g * 256 + h * 64
                rk = ZR_DK + g * 256 + h * 64
                ph.dma('sp', QT[g][:], zT[rq:rq + 64, :], w=[('QT', g)])
                ph.dma('sp', KT[g][:], zT[rk:rk + 64, :], w=[('KT', g)])
                P = d * n_ * 128
                for (j, t0) in ((0, 0), (n_ - 1, L - 64)):
                    for r0 in range(0, d, 8):
                        r1 = min(d, r0 + 8)
                        ph.dma('sp', bass.AP(Vg[g], (r0 * n_ + j) * 128, [[P, 64], [n_ * 128, r1 - r0], [1, 128]]),
                               bass.AP(dv_t, (t0 * d + r0) * 512 + h * 128, [[d * 512, 64], [512, r1 - r0], [1, 128]]),
                               w=[('Vg', g)])
                for r in range(d):
                    for j0 in range(1, n_ - 1, 8):
                        j1 = min(n_ - 1, j0 + 8)
                        ph.dma('sp', bass.AP(Vg[g], (r * n_ + j0) * 128, [[P, 128], [128, j1 - j0], [1, 128]]),
                               bass.AP(dv_t, ((128 * j0 - 64) * d + r) * 512 + h * 128,
                                       [[d * 512, 128], [128 * d * 512, j1 - j0], [1, 128]]),
                               w=[('Vg', g)])
            blocks = []

            def tail(bk):
                g, r, c, b, i3, nA, nB = bk
                d = DIL[g]
                rv = [('PT', i3), ('Vg', g)]
                MM(ph, pO[b][:, 0:128], Vg[g][0:nA, r, c, :], PT[i3][0:nA, 0:128], True, False, rv, [('pO', b)])
                MM(ph, pO[b][:, 0:128], Vg[g][0:nB, r, c + 1, :], PT[i3][0:nB, 128:256], False, True, rv, [('pO', b)])
                MM(ph, pO[b][:, 128:256], onesb[0:nA, :], PT[i3][0:nA, 0:128], True, False, rv, [('pO', b)])
                MM(ph, pO[b][:, 128:256], onesb[0:nB, :], PT[i3][0:nB, 128:256], False, True, rv, [('pO', b)])
                span = 128 * d
                k0 = (128 * c * d) // 512
                keys = [('acc', k) for k in range(k0, max(k0 + 1, (128 * c * d + span + 511) // 512))]
                outv = PV_(acc, 0, 128, 128 * c * d + r, [[S, 2], [d, 128]])
                inv_ = PV_(pO[b], 0, 128, 0, [[128, 2], [1, 128]])
                if g == 0:
                    CP(ph, 'dve', outv, inv_, [], keys, x=[('pO', b)])
                else:
                    TT(ph, 'dve', outv, inv_, outv, ALU.add, keys, keys, x=[('pO', b)])

            for g in range(3):
                d = DIL[g]
                L = S // d
                nq = L // 128
                eb = (g * 4 + h) * 3
                for r in range(d):
                    for c in range(nq):
                        b = blk % 2
                        i3 = blk % 3
                        blk += 1
                        qAP = PV_(QT[g], 0, 64, 128 * c * d + r, [[d, 128]])
                        nA = 64 if c == 0 else 128
                        tA = 0 if c == 0 else 128 * c - 64
                        nB = 64 if c == nq - 1 else 128
                        tB = 128 * c + 64
                        kA = PV_(KT[g], 0, 64, tA * d + r, [[d, nA]])
                        kB = PV_(KT[g], 0, 64, tB * d + r, [[d, nB]])
                        EA = dE[0:nA, eb + (2 if c == 0 else 0), :]
                        EB = dE[0:nB, eb + 1, :]
                        rr = [('QT', g), ('KT', g)]
                        MM(ph, pS[b][0:nA, 0:128], kA, qAP, True, True, rr, [('pS', b)])
                        MM(ph, pS[b][0:nB, 128:256], kB, qAP, True, True, rr, [('pS', b)])
                        if nA == 128 and nB == 128:
                            TT(ph, 'dve', sbias[i3][:], pS[b][:, 0:256], PV_(dE, 0, 128, eb * 128, [[1, 256]]), ALU.add,
                               [], [('sb', i3)], x=[('pS', b)])
                        else:
                            TT(ph, 'dve', sbias[i3][0:nA, 0:128], pS[b][0:nA, 0:128], EA, ALU.add, [], [('sb', i3)],
                               x=[('pS', b)])
                            TT(ph, 'dve', sbias[i3][0:nB, 128:256], pS[b][0:nB, 128:256], EB, ALU.add, [], [('sb', i3)],
                               x=[('pS', b)])
                        ACTF(ph, PT[i3][:], sbias[i3][:], AF.Exp, [('sb', i3)], [('PT', i3)], scale=0.125)
                        blocks.append((g, r, c, b, i3, nA, nB))
                        if len(blocks) >= 2:
                            tail(blocks[-2])
            tail(blocks[-1])
            for qt in range(NTT):
                q2 = qt % 2
                RECIP(ph, rz[q2][:], acc[:, 1, qt * 512:(qt + 1) * 512], [('acc', qt)], [('rz', q2)])
                TT(ph, 'pool', ost[q2][:], acc[:, 0, qt * 512:(qt + 1) * 512], rz[q2][:], ALU.mult,
                   [('acc', qt), ('rz', q2)], [('ost', q2)])
                ph.dma('pool', obT[h * 128:(h + 1) * 128, qt * 512:(qt + 1) * 512], ost[q2][:], r=[('ost', q2)])
        ph.emit()


def phase_diff(nc, name, layer, zT, fvn, fE, farb, lamv_d, gsub_row, ocT, onesb, onesf, ident):
    lam_init = 0.8 - 0.6 * math.exp(-0.3 * layer)
    ph = Phase(nc, name)
    with ExitStack() as st:
        sb = lambda n, shp, dt: st.enter_context(nc.sbuf_tensor(f"{name}_{n}", shp, dt))
        pst = lambda n, shp, dt: st.enter_context(nc.psum_tensor(f"{name}_{n}", shp, dt))
        QT = [sb(f"QT{i}", [128, S], BF16) for i in range(2)]
        KT = [sb(f"KT{i}", [128, S], BF16) for i in range(2)]
        Vf = [sb(f"Vf{i}", [128, NT, 128], BF16) for i in range(2)]
        PT = [[sb(f"PT{m}_{i}", [128, 512], BF16) for i in range(4)] for m in range(2)]
        zacc = [[sb(f"za{m}_{i}", [128, 512], F32) for i in range(2)] for m in range(2)]
        Zs = [sb(f"Zs{m}", [128, 512], F32) for m in range(2)]
        Os = [sb(f"Os{m}", [128, 512], F32) for m in range(2)]
        lv = sb("lv", [1, 4, 64], F32)
        lp = sb("lp", [1, 2, 64], F32)
        ls = sb("ls", [1, 2], F32)
        le = sb("le", [1, 2], F32)
        ll = sb("ll", [1, 1], F32)
        nl = sb("nl", [128, 1], F32)
        gs = sb("gs", [128, 1], F32)
        epst = sb("eps", [128, 1], F32)
        rA = sb("rA", [128, 512], F32)
        rB = sb("rB", [128, 512], F32)
        O_ = sb("O", [128, 512], F32)
        sq = sb("sq", [128, 512], BF16)
        lnr = sb("lnr", [128, 512], F32)
        rt = sb("rt", [128, 512], F32)
        res = [sb(f"res{i}", [128, 512], BF16) for i in range(2)]
        pS = [[pst(f"pS{m}_{i}", [128, 512], F32) for i in range(2)] for m in range(2)]
        pO = [pst(f"pO{i}", [128, 512], F32) for i in range(2)]
        pZ = [pst(f"pZ{i}", [128, 512], F32) for i in range(2)]
        pss = pZ[1]
        KSS = ('pZ', 1)
        MEMSET(ph, 'pool', epst[:], EPS, ['eps'])
        ph.dma('sp', lv[:], lamv_d, w=['lv'])
        ph.dma('sp', gs[:], bass.AP(gsub_row.tensor, gsub_row.offset, [[1, 128], [1, 1]]), w=['gs'])
        TT(ph, 'dve', lp[:], PV_(lv, 0, 1, 0, [[128, 2], [1, 64]]), PV_(lv, 0, 1, 64, [[128, 2], [1, 64]]), ALU.mult,
           ['lv'], ['lp'])
        ph.op('dve', lambda e: e.reduce_sum(out=ls[:], in_=lp[:], axis=AX.X), r=['lp'], w=['ls'])
        ACTF(ph, le[:], ls[:], AF.Exp, ['ls'], ['le'])
        TT(ph, 'dve', ll[:], le[:, 1:2], le[:, 0:1], ALU.subtract, ['le'], ['ll'])
        TS(ph, 'dve', ll[:], ll[:], -lam_init, None, ALU.add, None, ['ll'], ['ll'])
        MM(ph, pss[:, 0:1], onesf[0:1, :], ll[0:1, 0:1], True, True, ['ll'], [KSS])
        CP(ph, 'dve', nl[:], pss[:, 0:1], [], ['nl'], x=[KSS])
        TS(ph, 'dve', gs[:], gs[:], 1.0 - lam_init, None, ALU.mult, None, ['gs'], ['gs'])
        fv_t = fvn.tensor

        def loads(h):
            hs = h % 2
            for m in range(2):
                rq = ZR_FQ + m * 256 + h * 64
                rk = ZR_FK + m * 256 + h * 64
                ph.dma('sp', QT[hs][m * 64:(m + 1) * 64, :], zT[rq:rq + 64, :], w=[('QT', hs)])
                ph.dma('sp', KT[hs][m * 64:(m + 1) * 64, :], zT[rk:rk + 64, :], w=[('KT', hs)])
            for c0 in range(0, NT, 8):
                ph.dma('sp', Vf[hs][:, c0:c0 + 8, :],
                       bass.AP(fv_t, c0 * 128 * 512 + h * 128, [[512, 128], [128 * 512, 8], [1, 128]]), w=[('Vf', hs)])

        loads(0)
        fin, pend1, pend2 = [], [], []
        gq = 0
        for h in range(4):
            hs = h % 2
            if h < 3:
                loads(h + 1)
            for qt in range(NTT):
                zp = gq % 2
                gq += 1

                def pv(kc, hs=hs):
                    i4 = kc % 4
                    for m in range(2):
                        MM(ph, pO[m][:], Vf[hs][:, kc, :], PT[m][i4][:], kc == 0, kc == NT - 1,
                           [('PT', m, i4), ('Vf', hs)], [('pO', m)])
                        if kc % 2 == 1:
                            MM(ph, pZ[m][:], onesb[:], PT[m][i4][:], kc == 1, False, [('PT', m, i4)], [('pZ', m)])

                for kc in range(NT):
                    b = kc % 2
                    i4 = kc % 4
                    near = (4 * qt - 1 <= kc <= 4 * qt + 4)
                    for m in range(2):
                        MM(ph, pS[m][b][:], KT[hs][m * 64:(m + 1) * 64, kc * 128:(kc + 1) * 128],
                           QT[hs][m * 64:(m + 1) * 64, qt * 512:(qt + 1) * 512], True, not near,
                           [('KT', hs), ('QT', hs)], [('pS', m, b)])
                    if near:
                        for m in range(2):
                            col = m * 4 + h
                            for j in range(4):
                                off = kc - (4 * qt + j)
                                idx = off + 1 if abs(off) <= 1 else (3 if off < 0 else 4)
                                MM(ph, pS[m][b][:, j * 128:(j + 1) * 128], ident[:], fE[:, col * 5 + idx, :], False, j == 3,
                                   [], [('pS', m, b)])
                    for m in range(2):
                        col = m * 4 + h
                        if near:
                            ACTF(ph, PT[m][i4][:], pS[m][b][:], AF.Exp, [], [('PT', m, i4)], x=[('pS', m, b)], scale=0.125)
                        else:
                            side = 0 if kc < 4 * qt else 1
                            ACTF(ph, PT[m][i4][:], pS[m][b][:], AF.Exp, [], [('PT', m, i4)], x=[('pS', m, b)], scale=0.125,
                                 bias=farb[:, col * 2 + side:col * 2 + side + 1])
                        if kc % 2 == 0:
                            if kc == 0:
                                CP(ph, 'dve', zacc[m][zp][:], PT[m][i4][:], [('PT', m, i4)], [('za', m, zp)])
                            else:
                                TT(ph, 'dve', zacc[m][zp][:], zacc[m][zp][:], PT[m][i4][:], ALU.add,
                                   [('PT', m, i4), ('za', m, zp)], [('za', m, zp)])
                    if kc == 1 and fin:
                        fin.pop(0)()
                        if len(pend2) >= 2:
                            pend2.pop(0)()
                    if kc > 1:
                        pv(kc - 2)
                    if kc == 6 and pend1:
                        pend1.pop(0)()
                pv(NT - 2)
                pv(NT - 1)

                def finish(zp=zp):
                    for m in range(2):
                        MM(ph, pZ[m][:], onesf[:], zacc[m][zp][:], False, True, [('za', m, zp)], [('pZ', m)])
                    for m in range(2):
                        CP(ph, 'dve', Os[m][:], pO[m][:], [], [('Os', m)], x=[('pO', m)])
                    for m in range(2):
                        CP(ph, 'dve', Zs[m][:], pZ[m][:], [], [('Zs', m)], x=[('pZ', m)])

                def epi1():
                    RECIP(ph, rA[:], Zs[0][:], [('Zs', 0)], ['rA'])
                    RECIP(ph, rB[:], Zs[1][:], [('Zs', 1)], ['rB'])
                    TT(ph, 'dve', rA[:], Os[0][:], rA[:], ALU.mult, [('Os', 0), 'rA'], ['rA'])
                    TT(ph, 'dve', rB[:], Os[1][:], rB[:], ALU.mult, [('Os', 1), 'rB'], ['rB'])
                    STT(ph, 'dve', O_[:], rB[:], nl[:, 0:1], rA[:], ALU.mult, ALU.add, ['rA', 'rB', 'nl'], ['O'])
                    TT(ph, 'pool', sq[:], O_[:], O_[:], ALU.mult, ['O'], ['sq'])

                def epi2(h=h, qt=qt):
                    MM(ph, pss[:], onesb[:], sq[:], True, True, ['sq'], [KSS])
                    ACTF(ph, lnr[:], pss[:], AF.Ln, ['eps'], ['lnr'], x=[KSS], scale=1.0 / 128, bias=epst[:])
                    ACTF(ph, rt[:], lnr[:], AF.Exp, ['lnr'], ['rt'], scale=-0.5)
                    q2 = qt % 2
                    STT(ph, 'dve', res[q2][:], O_[:], gs[:, 0:1], rt[:], ALU.mult, ALU.mult, ['O', 'gs', 'rt'],
                        [('res', q2)])
                    ph.dma('pool', ocT[h * 128:(h + 1) * 128, qt * 512:(qt + 1) * 512], res[q2][:], r=[('res', q2)])

                fin.append(finish)
                pend1.append(epi1)
                pend2.append(epi2)
        for c in fin:
            c()
        if len(pend2) >= 2:
            pend2.pop(0)()
        for c in pend1:
            c()
        for c in pend2:
            c()
        ph.emit()


def precast_mlp_weights(ph, w_up_l, w_down_l, wub_d, wdb_d):
    for (src, dst) in ((w_up_l, wub_d), (w_down_l, wdb_d)):
        for q in range(4):
            ph.dma('pool', bass.AP(dst, q * 512 * 2048, [[2048, 512], [1, 2048]]),
                   bass.AP(src.tensor, src.offset + q * 512 * 2048, [[2048, 512], [1, 2048]]))


def norm_resid(ph, py, slot, xt, gpb, x_src_chunk, x_dst_chunk, tiles, epst, keyp):
    junk, ss, rs, tmp, xn = tiles['junk'][slot], tiles['ss'][slot], tiles['rs'][slot], tiles['tmp'][slot], tiles['xn'][slot]
    ph.dma('sp', xt[slot][:], x_src_chunk, w=[('xt', slot)])
    ACTF(ph, junk[:], py[:], AF.Square, [], [('junk', slot), ('ss', slot)], x=[keyp], accum_out=ss[:])
    ACTF(ph, rs[:], ss[:], AF.Sqrt, [('ss', slot), 'eps'], [('rs', slot)], scale=1.0 / D, bias=epst[:])
    RECIP(ph, rs[:], rs[:], [('rs', slot)], [('rs', slot)])
    STT(ph, 'dve', tmp[:], py[:], rs[:, 0:1], gpb[:], ALU.mult, ALU.mult, [('rs', slot), 'gpb'], [('tmp', slot)], x=[keyp])
    TT(ph, 'dve', xn[:], tmp[:], xt[slot][:], ALU.add, [('tmp', slot), ('xt', slot)], [('xn', slot)])
    ph.dma('pool', x_dst_chunk, xn[:], r=[('xn', slot)])


def phase_merge(nc, name, zT, oT3, w_branch_l, w_out_l, gpost_row, gmlp_row, x_src, x_dst, h2T_d, ident):
    ph = Phase(nc, name)
    with ExitStack() as st:
        sb = lambda n, shp, dt: st.enter_context(nc.sbuf_tensor(f"{name}_{n}", shp, dt))
        pst = lambda n, shp, dt: st.enter_context(nc.psum_tensor(f"{name}_{n}", shp, dt))
        wb = sb("wb", [128, 12, D], BF16)
        wo = sb("wo", [128, 8, D], BF16)
        gpb = sb("gpb", [128, D], F32)
        gmb = sb("gmb", [128, D], F32)
        epst = sb("eps", [128, 1], F32)
        oT = [sb(f"oT{b}", [128, 4, 512], BF16) for b in range(3)]
        gt = [sb(f"gt{i}", [128, 3, 512], BF16) for i in range(2)]
        tmpm = [[sb(f"tm{i}_{b}", [128, 512], F32) for b in range(3)] for i in range(2)]
        mT = [sb(f"mT{i}", [128, 8, 512], BF16) for i in range(2)]
        xt = [sb(f"xt{i}", [128, D], F32) for i in range(2)]
        tiles = {'junk': [sb(f"junk{i}", [128, D], BF16) for i in range(2)],
                 'ss': [sb(f"ss{i}", [128, 1], F32) for i in range(2)],
                 'rs': [sb(f"rs{i}", [128, 1], F32) for i in range(2)],
                 'tmp': [sb(f"tmp{i}", [128, D], F32) for i in range(2)],
                 'xn': [sb(f"xn{i}", [128, D], F32) for i in range(2)]}
        ss2 = [sb(f"ssb{i}", [128, 1], F32) for i in range(2)]
        rs2 = [sb(f"rsb{i}", [128, 1], F32) for i in range(2)]
        hb = [sb(f"hb{i}", [128, D], BF16) for i in range(2)]
        hst = [sb(f"hst{i}", [128, 8, 128], BF16) for i in range(2)]
        pM = [pst(f"pM{i}", [128, 512], F32) for i in range(2)]
        pY = [pst(f"pY{i}", [128, D], F32) for i in range(2)]
        psT = [pst(f"psT{i}", [128, 8, 128], BF16) for i in range(1)]
        MEMSET(ph, 'pool', epst[:], EPS, ['eps'])
        wbt = w_branch_l.tensor
        for b0 in range(0, 12, 4):
            ph.dma('pool', wb[:, b0:b0 + 4, :],
                   bass.AP(wbt, w_branch_l.offset + b0 * 128 * D, [[D, 128], [128 * D, 4], [1, D]]), w=['wb'])
        for b0 in range(0, 8, 4):
            ph.dma('pool', wo[:, b0:b0 + 4, :],
                   bass.AP(w_out_l.tensor, w_out_l.offset + b0 * 128 * D, [[D, 128], [128 * D, 4], [1, D]]), w=['wo'])
        ph.dma('sp', gpb[:], bcast_row(gpost_row), w=['gpb'])
        ph.dma('sp', gmb[:], bcast_row(gmlp_row), w=['gmb'])
        zt_t = zT.tensor
        mi = 0
        for tt in range(NTT):
            ms = tt % 2
            for b in range(3):
                ph.dma('sp', oT[b][:], bass.AP(oT3[b], tt * 512, [[S, 128], [128 * S, 4], [1, 512]]),
                       w=[('oT', b)])
            for mc in range(8):
                gsl = mc % 2
                ph.dma('sp', gt[gsl][:],
                       bass.AP(zt_t, (ZR_G + mc * 128) * S + tt * 512, [[S, 128], [8 * 128 * S, 3], [1, 512]]),
                       w=[('gt', gsl)])
                for b in range(3):
                    p = mi % 2
                    mi += 1
                    for kc in range(4):
                        MM(ph, pM[p][:], wb[:, b * 4 + kc, mc * 128:(mc + 1) * 128], oT[b][:, kc, :], kc == 0, kc == 3,
                           ['wb', ('oT', b)], [('pM', p)])
                    TT(ph, 'dve', tmpm[gsl][b][:], pM[p][:], gt[gsl][:, b, :], ALU.mult, [('gt', gsl)],
                       [('tm', gsl, b)], x=[('pM', p)])
                TT(ph, 'pool', tmpm[gsl][0][:], tmpm[gsl][0][:], tmpm[gsl][1][:], ALU.add,
                   [('tm', gsl, 0), ('tm', gsl, 1)], [('tm', gsl, 0)])
                TT(ph, 'pool', mT[ms][:, mc, :], tmpm[gsl][0][:], tmpm[gsl][2][:], ALU.add,
                   [('tm', gsl, 0), ('tm', gsl, 2)], [('mT', ms)])
            prev_tr = None

            def do_tr(tc, sl):
                for k in range(8):
                    TR(ph, psT[0][:, k, :], hb[sl][:, k * 128:(k + 1) * 128], ident[:], [('hb', sl)], ['psT'])
                CP(ph, 'act', hst[sl][:], psT[0][:], [], [('hst', sl)], x=['psT'])
                ph.dma('pool', h2T_d[:, :, tc * 128:(tc + 1) * 128], hst[sl][:], r=[('hst', sl)])

            for c4 in range(4):
                tc = tt * 4 + c4
                sl = tc % 2
                py = pY[sl]
                for half in range(2):
                    for kc in range(8):
                        MM(ph, py[:, half * 512:(half + 1) * 512], mT[ms][:, kc, c4 * 128:(c4 + 1) * 128],
                           wo[:, kc, half * 512:(half + 1) * 512], kc == 0, kc == 7, ['wo', ('mT', ms)], [('pY', sl)])
                if prev_tr is not None:
                    do_tr(*prev_tr)
                rows = slice(tc * 128, (tc + 1) * 128)
                norm_resid(ph, py, sl, xt, gpb, x_src[rows, :], x_dst[rows, :], tiles, epst, ('pY', sl))
                xn = tiles['xn'][sl]
                ACTF(ph, tiles['junk'][sl][:], xn[:], AF.Square, [('xn', sl)], [('junk', sl), ('ss2', sl)],
                     accum_out=ss2[sl][:])
                ACTF(ph, rs2[sl][:], ss2[sl][:], AF.Sqrt, [('ss2', sl), 'eps'], [('rs2', sl)], scale=1.0 / D, bias=epst[:])
                RECIP(ph, rs2[sl][:], rs2[sl][:], [('rs2', sl)], [('rs2', sl)])
                STT(ph, 'dve', hb[sl][:], xn[:], rs2[sl][:, 0:1], gmb[:], ALU.mult, ALU.mult,
                    [('xn', sl), ('rs2', sl), 'gmb'], [('hb', sl)])
                prev_tr = (tc, sl)
            do_tr(*prev_tr)
        ph.emit()


def phase_mlp(nc, name, w_up_l, w_down_l, wub_d, wdb_d, h2T_d, gpost_row, x_src, x_dst):
    DFF = 4 * D
    ph = Phase(nc, name)
    with ExitStack() as st:
        sb = lambda n, shp, dt: st.enter_context(nc.sbuf_tensor(f"{name}_{n}", shp, dt))
        pst = lambda n, shp, dt: st.enter_context(nc.psum_tensor(f"{name}_{n}", shp, dt))
        wd = sb("wd", [128, 32, D], BF16)
        aT = sb("aT", [128, 32, 512], BF16)
        h2 = [sb(f"h2{i}", [128, 8, 512], BF16) for i in range(2)]
        wu = [sb(f"wu{i}", [128, 8, 256], BF16) for i in range(2)]
        rl = [sb(f"rl{i}", [128, 512], F32) for i in range(2)]
        gpb = sb("gpb", [128, D], F32)
        epst = sb("eps", [128, 1], F32)
        xt = [sb(f"xt{i}", [128, D], F32) for i in range(2)]
        tiles = {'junk': [sb(f"junk{i}", [128, D], BF16) for i in range(2)],
                 'ss': [sb(f"ss{i}", [128, 1], F32) for i in range(2)],
                 'rs': [sb(f"rs{i}", [128, 1], F32) for i in range(2)],
                 'tmp': [sb(f"tmp{i}", [128, D], F32) for i in range(2)],
                 'xn': [sb(f"xn{i}", [128, D], F32) for i in range(2)]}
        pU = [pst(f"pU{i}", [128, 512], F32) for i in range(2)]
        pY = [pst(f"pY{i}", [128, D], F32) for i in range(2)]
        MEMSET(ph, 'pool', epst[:], EPS, ['eps'])
        ph.dma('sp', gpb[:], bcast_row(gpost_row), w=['gpb'])
        for c0 in range(0, 32, 8):
            ph.dma('sp', wd[:, c0:c0 + 8, :], bass.AP(wdb_d, c0 * 128 * D, [[D, 128], [128 * D, 8], [1, D]]),
                   w=['wd'])
        ui = 0
        for tt in range(NTT):
            hs = tt % 2
            ph.dma('sp', h2[hs][:], h2T_d[:, :, tt * 512:(tt + 1) * 512], w=[('h2', hs)])
            for fb in range(16):
                ws = fb % 2
                ph.dma('sp', wu[ws][:], bass.AP(wub_d, fb * 256, [[DFF, 128], [128 * DFF, 8], [1, 256]]),
                       w=[('wu', ws)])
                for fi in range(2):
                    fc = fb * 2 + fi
                    p = ui % 2
                    ui += 1
                    for kc in range(8):
                        MM(ph, pU[p][:], wu[ws][:, kc, fi * 128:(fi + 1) * 128], h2[hs][:, kc, :], kc == 0, kc == 7,
                           [('wu', ws), ('h2', hs)], [('pU', p)])
                    ACTF(ph, rl[p][:], pU[p][:], AF.Relu, [], [('rl', p)], x=[('pU', p)])
                    TT(ph, 'pool', aT[:, fc, :], rl[p][:], rl[p][:], ALU.mult, [('rl', p)], ['aT'])
            for c4 in range(4):
                tc = tt * 4 + c4
                sl = tc % 2
                py = pY[sl]
                for half in range(2):
                    for kc in range(32):
                        MM(ph, py[:, half * 512:(half + 1) * 512], aT[:, kc, c4 * 128:(c4 + 1) * 128],
                           wd[:, kc, half * 512:(half + 1) * 512], kc == 0, kc == 31, ['wd', 'aT'], [('pY', sl)])
                rows = slice(tc * 128, (tc + 1) * 128)
                norm_resid(ph, py, sl, xt, gpb, x_src[rows, :], x_dst[rows, :], tiles, epst, ('pY', sl))
        ph.emit()

def build(nlayers=DEPTH, stop_after=None, dbg=()):
    nc = bass.Bass("TRN2", target_bir_lowering=False)

    def din(name, shape, dt=F32):
        return nc.dram_tensor(name, list(shape), dt, kind="ExternalInput")

    def dscr(name, shape, dt):
        kind = "ExternalOutput" if name in dbg else "Internal"
        return nc.dram_tensor(name, list(shape), dt, kind=kind)

    x = din("x", [S, D])
    pos_d = din("pos", [128, NT], I32)
    invf_d = din("invf", [128, 16])
    dbias_d = din("dbias", [128, 36, 128])
    fbias_d = din("fbias", [128, 40, 128])
    farb_d = din("farb", [128, 16])
    lamv_d = din("lamv", [DEPTH, 4, 64])
    w_in = din("w_in", [DEPTH, D, IN_COLS])
    g_mix_pre = din("g_mix_pre", [DEPTH, D])
    g_q = din("g_q", [DEPTH, 256])
    g_kv = din("g_kv", [DEPTH, 128])
    w_uq = din("w_uq", [DEPTH, 256, 768])
    w_ukv = din("w_ukv", [DEPTH, 128, 1024])
    g_diff_sub = din("g_diff_sub", [DEPTH, 128])
    w_branch = din("w_branch", [DEPTH, 3 * 512, D])
    w_out = din("w_out", [DEPTH, D, D])
    g_mix_post = din("g_mix_post", [DEPTH, D])
    g_mlp_pre = din("g_mlp_pre", [DEPTH, D])
    w_up = din("w_up", [DEPTH, D, 4 * D])
    w_down = din("w_down", [DEPTH, 4 * D, D])
    g_mlp_post = din("g_mlp_post", [DEPTH, D])
    ident_d = din("ident", [128, 128], BF16)
    y = nc.dram_tensor("y", [S, D], F32, kind="ExternalOutput")

    zT = dscr("zT", [ZT_ROWS, S], BF16)
    lat = dscr("lat", [S, 416], F32)
    dvn = dscr("dvn", [S, 512], BF16)
    fvn = dscr("fvn", [S, 512], BF16)
    oaT = dscr("oaT", [512, S], BF16)
    obT = dscr("obT", [512, S], BF16)
    ocT = dscr("ocT", [512, S], BF16)
    xa = dscr("xa", [S, D], F32)
    xb = dscr("xb", [S, D], F32)
    h2T_d = dscr("h2T_d", [128, 8, S], BF16)
    wub_d = dscr("wub_d", [D, 4 * D], BF16)
    wdb_d = dscr("wdb_d", [4 * D, D], BF16)
    hT_d = dscr("hT_d", [128, 8, S], BF16)
    cs_d = dscr("cs_d", [128, 2, NT, 16], F32)

    with ExitStack() as top:
        sbt = lambda n, shp, dt: top.enter_context(nc.sbuf_tensor(n, shp, dt))
        ident = sbt("ident_sb", [128, 128], BF16)
        onesb = sbt("onesb", [128, 128], BF16)
        onesf = sbt("onesf", [128, 128], F32)
        cosT = sbt("cosT", [128, NT, 16], F32)
        sinT = sbt("sinT", [128, NT, 16], F32)
        dE = sbt("dE_sb", [128, 36, 128], F32)
        fE = sbt("fE_sb", [128, 40, 128], BF16)
        farb = sbt("farb_sb", [128, 16], F32)
        phase_init(nc, "init", ident, ident_d.ap(), pos_d.ap(), invf_d.ap(), cosT, sinT, onesb, onesf)
        phase_bias(nc, "bias", dbias_d.ap(), fbias_d.ap(), farb_d.ap(), dE, fE, farb)
        if "cs_d" in dbg:
            ph = Phase(nc, "dbgcs")
            ph.dma('sp', cs_d[:, 0, :, :], cosT[:])
            ph.dma('sp', cs_d[:, 1, :, :], sinT[:])
            ph.emit()
        done = False
        for l in range(nlayers):
            if stop_after == "I":
                break
            x_src = x.ap() if l == 0 else xb.ap()
            x_out = y.ap() if l == nlayers - 1 else xb.ap()
            with nc.sbuf_tensor(f"hT{l}", [128, 8, S], BF16) as hT:
                phase_win(nc, f"L{l}B", w_in[l], hT, zT, lat, dvn, fvn,
                          pre=lambda ph, st, hook: phase_norm_T(nc, f"L{l}A", x_src, g_mix_pre[l, :], hT, ident, ph=ph, st=st,
                                                               after_chunk=hook))
                if stop_after == "B":
                    break
            if stop_after not in ("E", "F") or True:
                with ExitStack() as stl:
                    cqnT = stl.enter_context(nc.sbuf_tensor(f"cqnT{l}", [128, 2, S], BF16))
                    ckvnT = stl.enter_context(nc.sbuf_tensor(f"ckvnT{l}", [128, S], BF16))
                    krT = stl.enter_context(nc.sbuf_tensor(f"krT{l}", [128, S], BF16))
                    phase_latprep(nc, f"L{l}C", lat.ap(), g_q[l, :], g_kv[l, :], cosT, sinT, cqnT, ckvnT, krT, ident)
                    if stop_after == "C":
                        break
                    if not SKIP_MLA:
                        phase_mla(nc, f"L{l}D", w_uq[l], w_ukv[l], cqnT, ckvnT, krT, cosT, sinT, oaT, ident, onesf,
                                  extra=lambda ph: precast_mlp_weights(ph, w_up[l], w_down[l], wub_d, wdb_d))
                if stop_after == "D":
                    break
            phase_dil(nc, f"L{l}E", zT.ap(), dvn.ap(), dE, obT, onesb, ident)
            if stop_after == "E":
                break
            phase_diff(nc, f"L{l}F", l, zT.ap(), fvn.ap(), fE, farb,
                       bass.AP(lamv_d, l * 256, [[256, 1], [64, 4], [1, 64]]), g_diff_sub[l, :], ocT, onesb, onesf, ident)
            if stop_after == "F":
                break
            phase_merge(nc, f"L{l}G", zT.ap(), (oaT, obT, ocT), w_branch[l], w_out[l], g_mix_post[l, :],
                        g_mlp_pre[l, :], x_src, xa.ap(), h2T_d, ident)
            if stop_after == "G":
                break
            phase_mlp(nc, f"L{l}H", w_up[l], w_down[l], wub_d, wdb_d, h2T_d, g_mlp_post[l, :], xa.ap(), x_out)
            if l == nlayers - 1:
                done = True
        if not done:
            ph = Phase(nc, "fin")
            ph.dma('sp', y.ap(), x.ap())
            ph.emit()
    return nc


SKIP_MLA = False


def _rel_bucket_np(rel):
    rel = np.asarray(rel, dtype=np.int64)
    nb, max_exact = 16, 8
    ret = np.where(rel > 0, nb, 0)
    n = np.abs(rel)
    try:
        import jax
        import jax.numpy as jnp
        with jax.default_device(jax.devices("cpu")[0]):
            nn = jnp.asarray(n.astype(np.int32))
            large = max_exact + (jnp.log(jnp.maximum(nn, 1).astype(jnp.float32) / max_exact)
                                 / math.log(128 / max_exact) * (nb - max_exact)).astype(jnp.int32)
            large = np.asarray(large).astype(np.int64)
    except Exception:
        lf = (np.log(np.maximum(n, 1).astype(np.float32) / np.float32(max_exact)).astype(np.float32)
              / np.float32(math.log(128 / max_exact))).astype(np.float32) * np.float32(nb - max_exact)
        large = max_exact + lf.astype(np.int32).astype(np.int64)
    large = np.minimum(large, nb - 1)
    return ret + np.where(n < max_exact, n, large)


def host_consts(rel_bias=None):
    invf = (10000.0 ** (-np.arange(16, dtype=np.float32) / 16)).astype(np.float32)
    out = {"ident": np.eye(128, dtype=np.float32).astype(ml_dtypes.bfloat16),
           "invf": np.ascontiguousarray(np.broadcast_to(invf[None, :], (128, 16)))}
    if rel_bias is None:
        rel_bias = np.zeros((32, 20), np.float32)
    rb = np.asarray(rel_bias, dtype=np.float32)
    MASK = np.float32(-3750.0)
    k = np.arange(128)[:, None]
    q = np.arange(128)[None, :]
    dbias = np.full((128, 36, 128), MASK, np.float32)
    for g in range(3):
        d = DIL[g]
        for h in range(4):
            col = g * 4 + h
            for ti, m in enumerate((k - 64 - q, k + 64 - q, k - q)):
                val = rb[_rel_bucket_np(m * d), col]
                dbias[:, col * 3 + ti, :] = np.where(np.abs(m) <= 64, val, MASK)
    fbias = np.zeros((128, 40, 128), np.float32)
    farb = np.zeros((128, 16), np.float32)
    for c8 in range(8):
        col = 12 + c8
        for idx, off in enumerate((-1, 0, 1)):
            fbias[:, c8 * 5 + idx, :] = rb[_rel_bucket_np(off * 128 + k - q), col]
        fbias[:, c8 * 5 + 3, :] = rb[15, col]
        fbias[:, c8 * 5 + 4, :] = rb[31, col]
        farb[:, c8 * 2 + 0] = rb[15, col]
        farb[:, c8 * 2 + 1] = rb[31, col]
    out.update({"dbias": dbias, "fbias": fbias, "farb": farb})
    return out


def core_inputs(inputs, c, consts):
    g = lambda k: np.asarray(inputs[k])
    m = {"x": np.ascontiguousarray(g("x")[c]),
         "pos": np.ascontiguousarray(g("positions")[c].astype(np.int32).reshape(NT, 128).T),
         "lamv": np.ascontiguousarray(np.stack([g("lam_q1"), g("lam_k1"), g("lam_q2"), g("lam_k2")], axis=1)),
         "w_in": g("w_in"), "g_mix_pre": g("g_mix_pre"),
         "g_q": g("g_q"), "g_kv": g("g_kv"),
         "w_uq": g("w_uq").reshape(DEPTH, 256, 768),
         "w_ukv": g("w_ukv").reshape(DEPTH, 128, 1024),
         "g_diff_sub": g("g_diff_sub"),
         "w_branch": g("w_branch").reshape(DEPTH, 3 * 512, D),
         "w_out": g("w_out"), "g_mix_post": g("g_mix_post"), "g_mlp_pre": g("g_mlp_pre"),
         "w_up": g("w_up"), "w_down": g("w_down"), "g_mlp_post": g("g_mlp_post")}
    m.update(consts)
    return m


def kernel(**inputs):
    nc = build()
    consts = host_consts(inputs["rel_bias"])
    in_maps = [core_inputs(inputs, c, consts) for c in range(NCORES)]
    res = run_bass_kernel_spmd(nc, in_maps, core_ids=list(range(NCORES)))
    return np.stack([np.asarray(r["y"]) for r in res.results], axis=0).astype(np.float32)
```

```python
import math
from contextlib import ExitStack

import numpy as np
import ml_dtypes

import concourse.bass as bass
import concourse.mybir as mybir
from concourse.bass_utils import run_bass_kernel_spmd

F32 = mybir.dt.float32
BF16 = mybir.dt.bfloat16
I32 = mybir.dt.int32
AF = mybir.ActivationFunctionType
ALU = mybir.AluOpType
AX = mybir.AxisListType

S = 4096
D = 1024
DEPTH = 4
NCORES = 8
IN_COLS = 7072
EPS = 1e-6
NT = S // 128
NTT = S // 512

C_CQ, C_CKV, C_KR, C_DQ, C_DK, C_DV, C_FQ, C_FK, C_FV, C_GZ = (
    0, 256, 384, 416, 1184, 1952, 2464, 2976, 3488, 4000)

ENG_ATTR = {'pe': 'tensor', 'act': 'scalar', 'dve': 'vector', 'pool': 'gpsimd', 'sp': 'sync'}
NDMA_SEMS = 6
VARIANT = ""
ROPE_ENG = "dve"
LAT_NT = None


class _Op:
    __slots__ = ('eng', 'fn', 'dma', 'deps', 'needed', 'tok', 'raw')

    def __init__(self, eng, fn, dma):
        self.eng = eng
        self.fn = fn
        self.dma = dma
        self.deps = []
        self.needed = False
        self.tok = None


class Phase:
    def __init__(self, nc, name):
        self.nc = nc
        self.name = name
        self.ops = {e: [] for e in ENG_ATTR}
        self.all_ops = []
        self.last_writer = {}
        self.readers = {}

    def op(self, eng, fn, r=(), w=(), dma=False, x=()):
        o = _Op(eng, fn, dma)
        deps = {}
        raw_keys = set(r) | set(x)
        r = list(r)
        w = list(w) + list(x)
        for k in raw_keys:
            d = self.last_writer.get(k)
            if d is not None:
                deps[id(d)] = (d, True)
        for k in w:
            d = self.last_writer.get(k)
            if d is not None and id(d) not in deps:
                deps[id(d)] = (d, False)
            for d in self.readers.get(k, ()):
                if id(d) not in deps:
                    deps[id(d)] = (d, False)
        for d, raw in deps.values():
            if d is o:
                continue
            if d.eng == eng and not d.dma:
                if eng == 'pe' or not raw:
                    continue
            o.deps.append(d)
        for k in w:
            self.last_writer[k] = o
            self.readers[k] = []
        for k in r:
            lst = self.readers.setdefault(k, [])
            if not dma:
                lst[:] = [d for d in lst if d.dma or d.eng != eng]
            lst.append(o)
        self.ops[eng].append(o)
        self.all_ops.append(o)
        return o

    def dma(self, q, out, in_, r=(), w=(), **kw):
        return self.op(q, lambda e: e.dma_start(out=out, in_=in_, **kw), r=r, w=w, dma=True)

    def emit(self):
        nc = self.nc
        allsems = []

        def newsem(nm):
            h = nc.alloc_semaphore(name=nm)
            allsems.append(h)
            return h

        dma_sems = {}
        dma_cnt = {}
        last_user = {}
        nd = {}
        for o in self.all_ops:
            if o.dma:
                q = o.eng
                if q not in dma_sems:
                    dma_sems[q] = [newsem(f"{self.name}_{q}_d{i}") for i in range(NDMA_SEMS)]
                    nd[q] = 0
                i = nd[q] % NDMA_SEMS
                nd[q] += 1
                key = (q, i)
                prev = last_user.get(key)
                if prev is not None:
                    o.deps.append(prev)
                last_user[key] = o
                dma_cnt[key] = dma_cnt.get(key, 0) + 16
                o.tok = (dma_sems[q][i], dma_cnt[key])
                o.needed = True
        for o in self.all_ops:
            for d in o.deps:
                d.needed = True
        eng_sem = {}
        eng_cnt = {}
        for o in self.all_ops:
            if o.dma or not o.needed:
                continue
            e = o.eng
            if e not in eng_sem:
                eng_sem[e] = newsem(f"{self.name}_{e}_c")
                eng_cnt[e] = 0
            eng_cnt[e] += 1
            o.tok = (eng_sem[e], eng_cnt[e])

        def make(e):
            ops = self.ops[e]

            def body(engobj):
                waited = {}
                for o in ops:
                    need = {}
                    for d in o.deps:
                        sem, val = d.tok
                        if need.get(id(sem), (None, 0))[1] < val:
                            need[id(sem)] = (sem, val)
                    for sem, val in need.values():
                        if waited.get(id(sem), 0) < val:
                            engobj.wait_ge(sem, val)
                            waited[id(sem)] = val
                    ins = o.fn(engobj)
                    if o.dma:
                        ins.then_inc(o.tok[0], 16)
                    elif o.needed:
                        ins.then_inc(o.tok[0], 1)
                for (q, i), cnt in dma_cnt.items():
                    if q == e:
                        sem = dma_sems[q][i]
                        if waited.get(id(sem), 0) < cnt:
                            engobj.wait_ge(sem, cnt)
            return body

        with nc.Block() as block:
            for e, attr in ENG_ATTR.items():
                if self.ops[e]:
                    getattr(block, attr)(make(e))
        nc.clear_and_free_semaphores(allsems)
        nc.all_engine_barrier()


def bcast_row(ap_row, nparts=128):
    t = ap_row.tensor
    n = ap_row.ap[-1][1]
    return bass.AP(t, ap_row.offset, [[0, nparts], [1, n]])


def phase_norm_T(nc, name, x_src, g_row, hT, ident, ph=None, st=None, after_chunk=None):
    own = ph is None
    if own:
        ph = Phase(nc, name)
        st = ExitStack()
    if True:
        sb = lambda n, shp, dt: st.enter_context(nc.sbuf_tensor(f"{name}_{n}", shp, dt))
        xt = [sb(f"xt{i}", [128, D], F32) for i in range(4)]
        gbc = sb("gbc", [128, D], F32)
        junk = [sb(f"junk{i}", [128, D], BF16) for i in range(4)]
        ss = [sb(f"ss{i}", [128, 1], F32) for i in range(4)]
        rs = [sb(f"rs{i}", [128, 1], F32) for i in range(4)]
        hb = [sb(f"hb{i}", [128, D], BF16) for i in range(4)]
        psT = [st.enter_context(nc.psum_tensor(f"{name}_psT{i}", [128, 8, 128], BF16)) for i in range(2)]
        epst = sb("eps", [128, 1], F32)
        ph.op('pool', lambda e: e.memset(epst[:], EPS), w=['eps'])
        ph.dma('sp', gbc[:], bcast_row(g_row), w=['gbc'])
        for tc in range(NT):
            s = tc % 4
            p2 = tc % 2
            ph.dma('sp', xt[s][:], x_src[tc * 128:(tc + 1) * 128, :], w=[('xt', s)])
            ph.op('act', lambda e, s=s: e.activation(out=junk[s][:], in_=xt[s][:], func=AF.Square,
                                                     accum_out=ss[s][:]),
                  r=[('xt', s)], w=[('ss', s), ('junk', s)])
            ph.op('act', lambda e, s=s: e.activation(out=rs[s][:], in_=ss[s][:], func=AF.Sqrt,
                                                     scale=1.0 / D, bias=epst[:]),
                  r=[('ss', s), 'eps'], w=[('rs', s)])
            ph.op('dve', lambda e, s=s: e.reciprocal(out=rs[s][:], in_=rs[s][:]),
                  r=[('rs', s)], w=[('rs', s)])
            ph.op('dve', lambda e, s=s: e.scalar_tensor_tensor(out=hb[s][:], in0=xt[s][:],
                                                               scalar=rs[s][:, 0:1], in1=gbc[:],
                                                               op0=ALU.mult, op1=ALU.mult),
                  r=[('xt', s), ('rs', s), 'gbc'], w=[('hb', s)])
            for k in range(8):
                ph.op('pe', lambda e, s=s, k=k, p2=p2: e.transpose(psT[p2][:, k, :], hb[s][:, k * 128:(k + 1) * 128],
                                                            ident[:]),
                      r=[('hb', s)], w=[('psT', p2)])
            if tc % 2 == 0:
                ph.op('act', lambda e, p2=p2, tc=tc: e.copy(out=hT[:, :, tc * 128:(tc + 1) * 128], in_=psT[p2][:]),
                      r=[('psT', p2)], w=[('hT', tc)])
            else:
                ph.op('dve', lambda e, p2=p2, tc=tc: e.tensor_copy(out=hT[:, :, tc * 128:(tc + 1) * 128],
                                                                  in_=psT[p2][:]),
                      r=[('psT', p2)], w=[('hT', tc)])
            if after_chunk is not None:
                after_chunk(tc)
        if own:
            ph.emit()
            st.close()


ZT_ROWS = 5632
ZR_DQ, ZR_DK, ZR_FQ, ZR_FK, ZR_G = 0, 768, 1536, 2048, 2560


def zt_src_col(r):
    if r < 1536:
        return C_DQ + r
    if r < 2560:
        return C_FQ + (r - 1536)
    return C_GZ + (r - 2560)


def phase_win(nc, name, w_in_l, hT, zT, lat, dvn, fvn, pre=None):
    ph = Phase(nc, name)
    with ExitStack() as st:
        sb = lambda n, shp, dt: st.enter_context(nc.sbuf_tensor(f"{name}_{n}", shp, dt))
        wt = [sb(f"wt{i}", [128, 8, 256], BF16) for i in range(2)]
        wtok = sb("wtok", [128, 8, 1440], BF16)
        stage = [sb(f"stage{i}", [128, S], BF16) for i in range(2)]
        tstage = [sb(f"tstage{i}", [128, 1024], BF16) for i in range(2)]
        lstage = [sb(f"lstage{i}", [128, 416], F32) for i in range(2)]
        ps = [st.enter_context(nc.psum_tensor(f"{name}_ps{i}", [128, 512], F32)) for i in range(6)]
        wT = w_in_l.tensor
        woff = w_in_l.offset

        def wsrc(c0, n):
            return bass.AP(wT, woff + c0, [[IN_COLS, 128], [128 * IN_COLS, 8], [1, n]])

        ph.dma('pool', wtok[:, :, 0:416], wsrc(0, 416), w=['wtok'])
        ph.dma('pool', wtok[:, :, 416:928], wsrc(C_DV, 512), w=['wtok'])
        ph.dma('pool', wtok[:, :, 928:1440], wsrc(C_FV, 512), w=['wtok'])
        pi = 0
        nblk = ZT_ROWS // 256
        tok_iter = iter(range(NT))

        def tok_chunk(tc):
            nonlocal pi
            s = tc % 2
            for gi, (c0, n) in enumerate(((0, 416), (416, 512), (928, 512))):
                b = pi % 6
                pi += 1
                for kc in range(8):
                    ph.op('pe', lambda e, b=b, kc=kc, tc=tc, c0=c0, n=n: e.matmul(
                        ps[b][:, 0:n], hT[:, kc, tc * 128:(tc + 1) * 128], wtok[:, kc, c0:c0 + n],
                        start=(kc == 0), stop=(kc == 7)),
                        r=['wtok', ('hT', tc)], w=[('ps', b)])
                if gi == 0:
                    ph.op('dve', lambda e, b=b, s=s: e.tensor_copy(out=lstage[s][:], in_=ps[b][:, 0:416]),
                          r=[('ps', b)], w=[('lstage', s)])
                    ph.dma('sp', lat[tc * 128:(tc + 1) * 128, :], lstage[s][:], r=[('lstage', s)])
                else:
                    o0 = (gi - 1) * 512
                    ph.op('dve', lambda e, b=b, s=s, o0=o0: e.tensor_copy(out=tstage[s][:, o0:o0 + 512],
                                                                          in_=ps[b][:]),
                          r=[('ps', b)], w=[('tstage', s)])
                    dst = dvn if gi == 1 else fvn
                    ph.dma('sp', dst[tc * 128:(tc + 1) * 128, :], tstage[s][:, o0:o0 + 512],
                           r=[('tstage', s)])

        if pre is not None:
            pre(ph, st, None)
        for blk in range(nblk):
            ws = blk % 2
            r0 = blk * 256
            ph.dma('pool', wt[ws][:], wsrc(zt_src_col(r0), 256), w=[('wt', ws)])
            for mi in range(2):
                row = r0 + mi * 128
                ss_ = (blk * 2 + mi) % 2
                is_gate = row >= ZR_G
                for tt in range(NTT):
                    b = pi % 6
                    pi += 1
                    for kc in range(8):
                        ph.op('pe', lambda e, b=b, kc=kc, tt=tt, ws=ws, mi=mi: e.matmul(
                            ps[b][:], wt[ws][:, kc, mi * 128:(mi + 1) * 128], hT[:, kc, tt * 512:(tt + 1) * 512],
                            start=(kc == 0), stop=(kc == 7)),
                            r=[('wt', ws)] + [('hT', tt * 4 + j) for j in range(4)], w=[('ps', b)])
                    if is_gate:
                        ph.op('act', lambda e, b=b, ss_=ss_, tt=tt: e.activation(
                            out=stage[ss_][:, tt * 512:(tt + 1) * 512], in_=ps[b][:], func=AF.Sigmoid),
                            r=[('ps', b)], w=[('stage', ss_)])
                    elif tt % 2 == 0:
                        ph.op('act', lambda e, b=b, ss_=ss_, tt=tt: e.copy(
                            out=stage[ss_][:, tt * 512:(tt + 1) * 512], in_=ps[b][:]),
                            r=[('ps', b)], w=[('stage', ss_)])
                    else:
                        ph.op('dve', lambda e, b=b, ss_=ss_, tt=tt: e.tensor_copy(
                            out=stage[ss_][:, tt * 512:(tt + 1) * 512], in_=ps[b][:]),
                            r=[('ps', b)], w=[('stage', ss_)])
                ph.dma('sp', zT[row:row + 128, :], stage[ss_][:], r=[('stage', ss_)])
            ntok = 2 if blk < 10 else 1
            for _ in range(ntok):
                tc = next(tok_iter, None)
                if tc is not None:
                    tok_chunk(tc)
        for tc in tok_iter:
            tok_chunk(tc)
        ph.emit()


def MM(ph, out, lhsT, rhs, start, stop, r, w):
    ph.op('pe', lambda e: e.matmul(out, lhsT, rhs, start=start, stop=stop, skip_group_check=True), r=r, w=w)


def TR(ph, out, in_, ident, r, w):
    ph.op('pe', lambda e: e.transpose(out, in_, ident), r=r, w=w)


def ACTF(ph, out, in_, func, r, w, x=(), **kw):
    ph.op('act', lambda e: e.activation(out=out, in_=in_, func=func, **kw), r=r, w=w, x=x)


def CP(ph, eng, out, in_, r, w, x=()):
    if eng == 'act':
        ph.op('act', lambda e: e.copy(out=out, in_=in_), r=r, w=w, x=x)
    else:
        ph.op(eng, lambda e: e.tensor_copy(out=out, in_=in_), r=r, w=w, x=x)


def TT(ph, eng, out, in0, in1, op, r, w, x=()):
    ph.op(eng, lambda e: e.tensor_tensor(out=out, in0=in0, in1=in1, op=op), r=r, w=w, x=x)


def TS(ph, eng, out, in0, s1, s2, op0, op1, r, w):
    if op1 is None:
        ph.op(eng, lambda e: e.tensor_scalar(out=out, in0=in0, scalar1=s1, scalar2=None, op0=op0), r=r, w=w)
    else:
        ph.op(eng, lambda e: e.tensor_scalar(out=out, in0=in0, scalar1=s1, scalar2=s2, op0=op0, op1=op1),
              r=r, w=w)


def STT(ph, eng, out, in0, scalar, in1, op0, op1, r, w, x=()):
    ph.op(eng, lambda e: e.scalar_tensor_tensor(out=out, in0=in0, scalar=scalar, in1=in1, op0=op0, op1=op1),
          r=r, w=w, x=x)


def RECIP(ph, out, in_, r, w, x=()):
    ph.op('dve', lambda e: e.reciprocal(out=out, in_=in_), r=r, w=w, x=x)


def MEMSET(ph, eng, ap, val, w):
    ph.op(eng, lambda e: e.memset(ap, val), w=w)


def phase_init(nc, name, ident, ident_d, pos_d, invf_d, cosT, sinT, onesb, onesf):
    ph = Phase(nc, name)
    with ExitStack() as st:
        sb = lambda n, shp, dt: st.enter_context(nc.sbuf_tensor(f"{name}_{n}", shp, dt))
        posi = sb("posi", [128, NT], I32)
        posf = sb("posf", [128, NT], F32)
        invf = sb("invf", [128, 16], F32)
        ang = sb("ang", [128, NT, 16], F32)
        a = sb("a", [128, NT, 16], F32)
        u = sb("u", [128, NT, 16], F32)
        ki = sb("ki", [128, NT, 16], I32)
        kf = sb("kf", [128, NT, 16], F32)
        r_ = sb("r", [128, NT, 16], F32)
        s2 = sb("s2", [128, NT, 16], F32)
        s4 = sb("s4", [128, NT, 16], F32)
        ph.dma('sp', ident[:], ident_d, w=['ident'])
        ph.dma('sp', posi[:], pos_d, w=['posi'])
        ph.dma('sp', invf[:], invf_d, w=['invf'])
        MEMSET(ph, 'pool', onesb[:], 1.0, ['onesb'])
        MEMSET(ph, 'pool', onesf[:], 1.0, ['onesf'])
        CP(ph, 'dve', posf[:], posi[:], ['posi'], ['posf'])
        for c in range(NT):
            TS(ph, 'dve', ang[:, c, :], invf[:], posf[:, c:c + 1], None, ALU.mult, None, ['posf', 'invf'], ['ang'])
        for tab, shift, nm in ((sinT, 0.0, 'sinT'), (cosT, math.pi / 2, 'cosT')):
            TS(ph, 'dve', a[:], ang[:], shift, None, ALU.add, None, ['ang'], ['a'])
            TS(ph, 'dve', u[:], a[:], 1.0 / (2 * math.pi), None, ALU.mult, None, ['a'], ['u'])
            CP(ph, 'dve', ki[:], u[:], ['u'], ['ki'])
            CP(ph, 'dve', kf[:], ki[:], ['ki'], ['kf'])
            STT(ph, 'dve', r_[:], kf[:], -2 * math.pi, a[:], ALU.mult, ALU.add, ['kf', 'a'], ['r'])
            ACTF(ph, s2[:], r_[:], AF.Sin, ['r'], ['s2'], scale=0.5)
            ACTF(ph, s4[:], r_[:], AF.Sin, ['r'], ['s4'], scale=0.25)
            TT(ph, 'dve', u[:], s4[:], s4[:], ALU.mult, ['s4'], ['u'])
            TS(ph, 'dve', u[:], u[:], -2.0, 1.0, ALU.mult, ALU.add, ['u'], ['u'])
            STT(ph, 'dve', tab[:], s2[:], 2.0, u[:], ALU.mult, ALU.mult, ['s2', 'u'], [nm])
        ph.emit()


def phase_latprep(nc, name, lat, gq_row, gkv_row, cosT, sinT, cqnT, ckvnT, krT, ident):
    ph = Phase(nc, name)
    with ExitStack() as st:
        sb = lambda n, shp, dt: st.enter_context(nc.sbuf_tensor(f"{name}_{n}", shp, dt))
        lt = [sb(f"lt{i}", [128, 416], F32) for i in range(2)]
        gqb = sb("gqb", [128, 256], F32)
        gkvb = sb("gkvb", [128, 128], F32)
        junk = [sb(f"junk{i}", [128, 256], BF16) for i in range(2)]
        ssa = [sb(f"ssa{i}", [128, 1], F32) for i in range(2)]
        ssb = [sb(f"ssb{i}", [128, 1], F32) for i in range(2)]
        rsa = [sb(f"rsa{i}", [128, 1], F32) for i in range(2)]
        rsb = [sb(f"rsb{i}", [128, 1], F32) for i in range(2)]
        cb = [sb(f"cb{i}", [128, 384], BF16) for i in range(2)]
        kb = [sb(f"kb{i}", [128, 96], BF16) for i in range(2)]
        tq = [[sb(f"tq{i}_{j}", [128, 16], F32) for j in range(4)] for i in range(2)]
        epst = sb("eps", [128, 1], F32)
        psT = [st.enter_context(nc.psum_tensor(f"{name}_psT{i}", [128, 8, 128], BF16)) for i in range(2)]
        MEMSET(ph, 'pool', epst[:], EPS, ['eps'])
        for i in range(2):
            MEMSET(ph, 'pool', kb[i][:], 0.0, [('kb', i)])
        ph.dma('sp', gqb[:], bcast_row(gq_row), w=['gqb'])
        ph.dma('sp', gkvb[:], bcast_row(gkv_row), w=['gkvb'])
        for tc in range(LAT_NT or NT):
            s = tc % 2
            L = lt[s]
            ph.dma('sp', L[:], lat[tc * 128:(tc + 1) * 128, :], w=[('lt', s)])
            ACTF(ph, junk[s][:, 0:256], L[:, 0:256], AF.Square, [('lt', s)], [('junk', s), ('ssa', s)],
                 accum_out=ssa[s][:])
            ACTF(ph, junk[s][:, 128:256], L[:, 256:384], AF.Square, [('lt', s)], [('junkb', s), ('ssb', s)],
                 accum_out=ssb[s][:])
            ACTF(ph, rsa[s][:], ssa[s][:], AF.Sqrt, [('ssa', s), 'eps'], [('rsa', s)],
                 scale=1.0 / 256, bias=epst[:])
            ACTF(ph, rsb[s][:], ssb[s][:], AF.Sqrt, [('ssb', s), 'eps'], [('rsb', s)],
                 scale=1.0 / 128, bias=epst[:])
            RECIP(ph, rsa[s][:], rsa[s][:], [('rsa', s)], [('rsa', s)])
            RECIP(ph, rsb[s][:], rsb[s][:], [('rsb', s)], [('rsb', s)])
            STT(ph, 'dve', cb[s][:, 0:256], L[:, 0:256], rsa[s][:, 0:1], gqb[:], ALU.mult, ALU.mult,
                [('lt', s), ('rsa', s), 'gqb'], [('cb', s)])
            STT(ph, 'dve', cb[s][:, 256:384], L[:, 256:384], rsb[s][:, 0:1], gkvb[:], ALU.mult, ALU.mult,
                [('lt', s), ('rsb', s), 'gkvb'], [('cb', s)])
            if VARIANT == "norope":
                continue
            x1 = L[:, 384:400]
            x2 = L[:, 400:416]
            c_ = cosT[:, tc, :]
            s_ = sinT[:, tc, :]
            t1, t2, t3, t4 = tq[s]
            TT(ph, ROPE_ENG, t2[:], x2, s_, ALU.mult, [('lt', s)], [('t2', s)])
            TT(ph, ROPE_ENG, t4[:], x1, s_, ALU.mult, [('lt', s)], [('t4', s)])
            TT(ph, 'dve', t1[:], x1, c_, ALU.mult, [('lt', s)], [('t1', s)])
            TT(ph, 'dve', t3[:], x2, c_, ALU.mult, [('lt', s)], [('t3', s)])
            if VARIANT != "R1":
                TT(ph, 'dve', kb[s][:, 64:80], t1[:], t2[:], ALU.subtract, [('t1', s), ('t2', s)], [('kb', s)])
            if VARIANT not in ("R1", "R2"):
                TT(ph, 'dve', kb[s][:, 80:96], t3[:], t4[:], ALU.add, [('t3', s), ('t4', s)], [('kb', s)])
            if VARIANT == "notr":
                continue
            for j in range(3):
                TR(ph, psT[s][:, j, :], cb[s][:, j * 128:(j + 1) * 128], ident[:], [('cb', s)], [('psT', s)])
            if VARIANT not in ("nokb", "noact", "nodve", "nocp"):
                TR(ph, psT[s][0:96, 3, :], kb[s][:, 0:96], ident[:], [('kb', s)], [('psT', s)])
            sl = slice(tc * 128, (tc + 1) * 128)
            if VARIANT not in ("noact", "nocp"):
                CP(ph, 'act', cqnT[:, :, sl], psT[s][:, 0:2, :], [], [('cqnT', tc)], x=[('psT', s)])
            if VARIANT not in ("nodve", "nocp"):
                CP(ph, 'dve', ckvnT[:, sl], psT[s][:, 2, :], [], [('ckvnT', tc)], x=[('psT', s)])
            if VARIANT not in ("nokb", "noact", "nodve", "nocp"):
                CP(ph, 'act', krT[64:96, sl], psT[s][64:96, 3, :], [], [('krT', tc)], x=[('psT', s)])
        ph.emit()


def PV_(t, p0, npart, off, dims):
    P = 1
    for d in t.shape[1:]:
        P *= d
    return bass.AP(t, p0 * P + off, [[P, npart]] + [list(d) for d in dims])


def phase_mla(nc, name, w_uq_l, w_ukv_l, cqnT, ckvnT, krT, cosT, sinT, oaT, ident, onesf, extra=None):
    ph = Phase(nc, name)
    with ExitStack() as st:
        sb = lambda n, shp, dt: st.enter_context(nc.sbuf_tensor(f"{name}_{n}", shp, dt))
        pst = lambda n, shp, dt: st.enter_context(nc.psum_tensor(f"{name}_{n}", shp, dt))
        wuq = sb("wuq", [128, 2, 768], BF16)
        wukv = sb("wukv", [128, 1024], BF16)
        Vall = sb("Vall", [128, NT, 8, 128], BF16)
        KT = [sb(f"KT{i}", [128, S], BF16) for i in range(2)]
        QT = [sb(f"QT{i}", [128, S], BF16) for i in range(2)]
        qb = [sb(f"qb{i}", [128, 4, 96], BF16) for i in range(2)]
        tq = [[sb(f"tq{i}_{j}", [128, 4, 16], F32) for j in range(4)] for i in range(2)]
        PT = [sb(f"PT{i}", [128, 512], BF16) for i in range(4)]
        rz = [sb(f"rz{i}", [128, 512], F32) for i in range(2)]
        bcs = [sb(f"bcs{i}", [64, 512], F32) for i in range(2)]
        ost = [sb(f"ost{i}", [64, 512], BF16) for i in range(2)]
        pS = [pst(f"pS{i}", [128, 512], F32) for i in range(4)]
        pO = [pst(f"pO{i}", [128, 512], F32) for i in range(2)]
        pG = pst("pG", [128, 512], F32)
        psQT = pst("psQT", [128, 8, 128], BF16)
        t_uq = w_uq_l.tensor
        ph.dma('pool', wuq[:], bass.AP(t_uq, w_uq_l.offset, [[768, 128], [128 * 768, 2], [1, 768]]), w=['wuq'])
        ph.dma('pool', wukv[:], bass.AP(w_ukv_l.tensor, w_ukv_l.offset, [[1024, 128], [1, 1024]]), w=['wukv'])
        MEMSET(ph, 'pool', Vall[:, :, :, 64:128], 1.0, ['Vall1'])
        if extra is not None:
            extra(ph)
        for c in range(NT):
            MM(ph, PV_(pG, 0, 128, 0, [[64, 8], [1, 64]]), ckvnT[:, c * 128:(c + 1) * 128],
               PV_(wukv, 0, 128, 64, [[128, 8], [1, 64]]), True, True, ['wukv'], ['pG'])
            CP(ph, 'dve', Vall[:, c, :, 0:64], PV_(pG, 0, 128, 0, [[64, 8], [1, 64]]), [], ['Vall'], x=['pG'])
        sc = 96 ** -0.5

        def kproj(h, s, tt):
            MM(ph, pG[0:64, :], wukv[:, h * 128:h * 128 + 64], ckvnT[:, tt * 512:(tt + 1) * 512], True, True,
               ['wukv'], ['pG'])
            CP(ph, 'dve', KT[s][0:64, tt * 512:(tt + 1) * 512], pG[0:64, :], [], [('KT', s)], x=['pG'])

        def q_tr(s, tg, b2):
            for c4 in range(4):
                TR(ph, psQT[0:96, c4, :], qb[b2][:, c4, :], ident[:], [('qb', b2)], ['psQT'])
            CP(ph, 'dve', PV_(QT[s], 0, 96, tg * 512, [[128, 4], [1, 128]]), psQT[0:96, 0:4, :], [], [('QT', s)],
               x=['psQT'])

        def qproj(h, s, tg):
            b2 = tg % 2
            gv = lambda a, b: PV_(pG, 0, 128, a, [[96, 4], [1, b]])
            for c4 in range(4):
                tc = tg * 4 + c4
                for j in range(2):
                    MM(ph, PV_(pG, 0, 128, c4 * 96, [[1, 96]]), cqnT[:, j, tc * 128:(tc + 1) * 128],
                       wuq[:, j, h * 96:(h + 1) * 96], (c4 == 0 and j == 0), (j == 1), ['wuq'], ['pG'])
            if tg > 0:
                q_tr(s, tg - 1, (tg - 1) % 2)
            x1 = gv(64, 16)
            x2 = gv(80, 16)
            c_ = cosT[:, tg * 4:(tg + 1) * 4, :]
            s_ = sinT[:, tg * 4:(tg + 1) * 4, :]
            t1, t2, t3, t4 = tq[b2]
            TT(ph, 'dve', t1[:], x1, c_, ALU.mult, [], [('t1', b2)], x=['pG'])
            TT(ph, 'dve', t2[:], x2, s_, ALU.mult, [], [('t2', b2)], x=['pG'])
            TT(ph, 'dve', t3[:], x2, c_, ALU.mult, [], [('t3', b2)], x=['pG'])
            TT(ph, 'dve', t4[:], x1, s_, ALU.mult, [], [('t4', b2)], x=['pG'])
            CP(ph, 'dve', qb[b2][:, :, 0:64], gv(0, 64), [], [('qb', b2)], x=['pG'])
            TT(ph, 'pool', qb[b2][:, :, 64:80], t1[:], t2[:], ALU.subtract, [('t1', b2), ('t2', b2)], [('qb', b2)])
            TT(ph, 'pool', qb[b2][:, :, 80:96], t3[:], t4[:], ALU.add, [('t3', b2), ('t4', b2)], [('qb', b2)])

        def make_prologue(h):
            s = h % 2
            ch = [(lambda tt=tt: kproj(h, s, tt)) for tt in range(NTT)]
            ch.append(lambda: CP(ph, 'pool', KT[s][64:96, :], krT[64:96, :], [], [('KT', s)]))
            ch += [(lambda tg=tg: qproj(h, s, tg)) for tg in range(8)]
            ch.append(lambda: q_tr(s, 7, 1))
            return ch

        for c in make_prologue(0):
            c()
        pending = []
        for h in range(8):
            s = h % 2
            nxt = make_prologue(h + 1) if h < 7 else []
            for qt in range(NTT):
                po = pO[qt % 2]
                q2 = qt % 2
                rhsQ = QT[s][0:96, qt * 512:(qt + 1) * 512]

                def pv(kc):
                    i4 = kc % 4
                    MM(ph, po[:, :], Vall[:, kc, h, :], PT[i4][:], kc == 0, kc == NT - 1,
                       [('PT', i4), 'Vall', 'Vall1'], [('pO', q2)])

                for kc in range(NT):
                    b = kc % 4
                    MM(ph, pS[b][:], KT[s][0:96, kc * 128:(kc + 1) * 128], rhsQ, True, True,
                       [('KT', s), ('QT', s)], [('pS', b)])
                    ACTF(ph, PT[b][:], pS[b][:], AF.Exp, [('pS', b)], [('PT', b)], scale=sc)
                    if kc > 1:
                        pv(kc - 2)
                    if kc == 10 and pending:
                        pending.pop(0)()
                    if kc % 8 == 4 and nxt:
                        nxt.pop(0)()
                pv(NT - 2)
                pv(NT - 1)
                RECIP(ph, rz[q2][64:65, :], po[64:65, :], [], [('rz', q2)], x=[('pO', q2)])

                def epi(h=h, qt=qt, q2=q2, po=po):
                    MM(ph, pG[0:64, :], onesf[64:65, 0:64], rz[q2][64:65, :], True, True, [('rz', q2)], ['pG'])
                    CP(ph, 'dve', bcs[q2][:], pG[0:64, :], [], [('bcs', q2)], x=['pG'])
                    TT(ph, 'dve', ost[q2][:], po[0:64, :], bcs[q2][:], ALU.mult, [('bcs', q2)], [('ost', q2)],
                       x=[('pO', q2)])
                    ph.dma('pool', oaT[h * 64:(h + 1) * 64, qt * 512:(qt + 1) * 512], ost[q2][:], r=[('ost', q2)])

                pending.append(epi)
            for c in nxt:
                c()
        for c in pending:
            c()
        ph.emit()


DIL = (1, 4, 16)


def split_dma(ph, q, out_fn, in_fn, n, step, **kw):
    for a in range(0, n, step):
        b = min(n, a + step)
        ph.dma(q, out_fn(a, b), in_fn(a, b), **kw)


def phase_bias(nc, name, dbias_d, fbias_d, farb_d, dE, fE, farb):
    ph = Phase(nc, name)
    with ExitStack() as st:
        sb = lambda n, shp, dt: st.enter_context(nc.sbuf_tensor(f"{name}_{n}", shp, dt))
        t1 = sb("t1", [128, 36, 128], F32)
        t2 = sb("t2", [128, 40, 128], F32)
        ph.dma('sp', t1[:], dbias_d, w=['t1'])
        ph.dma('sp', t2[:], fbias_d, w=['t2'])
        ph.dma('sp', farb[:], farb_d, w=['farb'])
        TS(ph, 'dve', dE[:], t1[:], 8.0, None, ALU.mult, None, ['t1'], ['dE'])
        TS(ph, 'dve', fE[:], t2[:], 8.0, None, ALU.mult, None, ['t2'], ['fE'])
        ph.emit()


def phase_dil(nc, name, zT, dvn, dE, obT, onesb, ident, extra=None):
    ph = Phase(nc, name)
    with ExitStack() as st:
        if extra is not None:
            extra(ph)
        sb = lambda n, shp, dt: st.enter_context(nc.sbuf_tensor(f"{name}_{n}", shp, dt))
        pst = lambda n, shp, dt: st.enter_context(nc.psum_tensor(f"{name}_{n}", shp, dt))
        QT = [sb(f"QT{g}", [64, S], BF16) for g in range(3)]
        KT = [sb(f"KT{g}", [64, S], BF16) for g in range(3)]
        nch = [S // d // 128 + 1 for d in DIL]
        Vg = [sb(f"Vg{g}", [128, DIL[g], nch[g], 128], BF16) for g in range(3)]
        acc = sb("acc", [128, 2, S], F32)
        PT = [sb(f"PT{i}", [128, 256], BF16) for i in range(3)]
        sbias = [sb(f"sbias{i}", [128, 256], F32) for i in range(3)]
        rz = [sb(f"rz{i}", [128, 512], F32) for i in range(2)]
        ost = [sb(f"ost{i}", [128, 512], BF16) for i in range(2)]
        pS = [pst(f"pS{i}", [128, 512], F32) for i in range(2)]
        pO = [pst(f"pO{i}", [128, 512], F32) for i in range(2)]
        zt_t = zT.tensor
        dv_t = dvn.tensor
        blk = 0
        for h in range(4):
            for g in range(3):
                d = DIL[g]
                L = S // d
                n_ = nch[g]
                rq = ZR_DQ + g * 256 + h * 64
                rk = ZR_DK + g * 256 + h * 64
                ph.dma('sp', QT[g][:], zT[rq:rq + 64, :], w=[('QT', g)])
                ph.dma('sp', KT[g][:], zT[rk:rk + 64, :], w=[('KT', g)])
                P = d * n_ * 128
                for (j, t0) in ((0, 0), (n_ - 1, L - 64)):
                    for r0 in range(0, d, 8):
                        r1 = min(d, r0 + 8)
                        ph.dma('sp', bass.AP(Vg[g], (r0 * n_ + j) * 128, [[P, 64], [n_ * 128, r1 - r0], [1, 128]]),
                               bass.AP(dv_t, (t0 * d + r0) * 512 + h * 128, [[d * 512, 64], [512, r1 - r0], [1, 128]]),
                               w=[('Vg', g)])
                for r in range(d):
                    for j0 in range(1, n_ - 1, 8):
                        j1 = min(n_ - 1, j0 + 8)
                        ph.dma('sp', bass.AP(Vg[g], (r * n_ + j0) * 128, [[P, 128], [128, j1 - j0], [1, 128]]),
                               bass.AP(dv_t, ((128 * j0 - 64) * d + r) * 512 + h * 128,
                                       [[d * 512, 128], [128 * d * 512, j1 - j0], [1, 128]]),
                               w=[('Vg', g)])
            blocks = []

            def tail(bk):
                g, r, c, b, i3, nA, nB = bk
                d = DIL[g]
                rv = [('PT', i3), ('Vg', g)]
                MM(ph, pO[b][:, 0:128], Vg[g][0:nA, r, c, :], PT[i3][0:nA, 0:128], True, False, rv, [('pO', b)])
                MM(ph, pO[b][:, 0:128], Vg[g][0:nB, r, c + 1, :], PT[i3][0:nB, 128:256], False, True, rv, [('pO', b)])
                MM(ph, pO[b][:, 128:256], onesb[0:nA, :], PT[i3][0:nA, 0:128], True, False, rv, [('pO', b)])
                MM(ph, pO[b][:, 128:256], onesb[0:nB, :], PT[i3][0:nB, 128:256], False, True, rv, [('pO', b)])
                span = 128 * d
                k0 = (128 * c * d) // 512
                keys = [('acc', k) for k in range(k0, max(k0 + 1, (128 * c * d + span + 511) // 512))]
                outv = PV_(acc, 0, 128, 128 * c * d + r, [[S, 2], [d, 128]])
                inv_ = PV_(pO[b], 0, 128, 0, [[128, 2], [1, 128]])
                if g == 0:
                    CP(ph, 'dve', outv, inv_, [], keys, x=[('pO', b)])
                else:
                    TT(ph, 'dve', outv, inv_, outv, ALU.add, keys, keys, x=[('pO', b)])

            for g in range(3):
                d = DIL[g]
                L = S // d
                nq = L // 128
                eb = (g * 4 + h) * 3
                for r in range(d):
                    for c in range(nq):
                        b = blk % 2
                        i3 = blk % 3
                        blk += 1
                        qAP = PV_(QT[g], 0, 64, 128 * c * d + r, [[d, 128]])
                        nA = 64 if c == 0 else 128
                        tA = 0 if c == 0 else 128 * c - 64
                        nB = 64 if c == nq - 1 else 128
                        tB = 128 * c + 64
                        kA = PV_(KT[g], 0, 64, tA * d + r, [[d, nA]])
                        kB = PV_(KT[g], 0, 64, tB * d + r, [[d, nB]])
                        EA = dE[0:nA, eb + (2 if c == 0 else 0), :]
                        EB = dE[0:nB, eb + 1, :]
                        rr = [('QT', g), ('KT', g)]
                        MM(ph, pS[b][0:nA, 0:128], kA, qAP, True, True, rr, [('pS', b)])
                        MM(ph, pS[b][0:nB, 128:256], kB, qAP, True, True, rr, [('pS', b)])
                        if nA == 128 and nB == 128:
                            TT(ph, 'dve', sbias[i3][:], pS[b][:, 0:256], PV_(dE, 0, 128, eb * 128, [[1, 256]]), ALU.add,
                               [], [('sb', i3)], x=[('pS', b)])
                        else:
                            TT(ph, 'dve', sbias[i3][0:nA, 0:128], pS[b][0:nA, 0:128], EA, ALU.add, [], [('sb', i3)],
                               x=[('pS', b)])
                            TT(ph, 'dve', sbias[i3][0:nB, 128:256], pS[b][0:nB, 128:256], EB, ALU.add, [], [('sb', i3)],
                               x=[('pS', b)])
                        ACTF(ph, PT[i3][:], sbias[i3][:], AF.Exp, [('sb', i3)], [('PT', i3)], scale=0.125)
                        blocks.append((g, r, c, b, i3, nA, nB))
                        if len(blocks) >= 2:
                            tail(blocks[-2])
            tail(blocks[-1])
            for qt in range(NTT):
                q2 = qt % 2
                RECIP(ph, rz[q2][:], acc[:, 1, qt * 512:(qt + 1) * 512], [('acc', qt)], [('rz', q2)])
                TT(ph, 'pool', ost[q2][:], acc[:, 0, qt * 512:(qt + 1) * 512], rz[q2][:], ALU.mult,
                   [('acc', qt), ('rz', q2)], [('ost', q2)])
                ph.dma('pool', obT[h * 128:(h + 1) * 128, qt * 512:(qt + 1) * 512], ost[q2][:], r=[('ost', q2)])
        ph.emit()


def phase_diff(nc, name, layer, zT, fvn, fE, farb, lamv_d, gsub_row, ocT, onesb, onesf, ident):
    lam_init = 0.8 - 0.6 * math.exp(-0.3 * layer)
    ph = Phase(nc, name)
    with ExitStack() as st:
        sb = lambda n, shp, dt: st.enter_context(nc.sbuf_tensor(f"{name}_{n}", shp, dt))
        pst = lambda n, shp, dt: st.enter_context(nc.psum_tensor(f"{name}_{n}", shp, dt))
        QT = [sb(f"QT{i}", [128, S], BF16) for i in range(2)]
        KT = [sb(f"KT{i}", [128, S], BF16) for i in range(2)]
        Vf = [sb(f"Vf{i}", [128, NT, 128], BF16) for i in range(2)]
        PT = [[sb(f"PT{m}_{i}", [128, 512], BF16) for i in range(4)] for m in range(2)]
        zacc = [[sb(f"za{m}_{i}", [128, 512], F32) for i in range(2)] for m in range(2)]
        Zs = [sb(f"Zs{m}", [128, 512], F32) for m in range(2)]
        Os = [sb(f"Os{m}", [128, 512], F32) for m in range(2)]
        lv = sb("lv", [1, 4, 64], F32)
        lp = sb("lp", [1, 2, 64], F32)
        ls = sb("ls", [1, 2], F32)
        le = sb("le", [1, 2], F32)
        ll = sb("ll", [1, 1], F32)
        nl = sb("nl", [128, 1], F32)
        gs = sb("gs", [128, 1], F32)
        epst = sb("eps", [128, 1], F32)
        rA = sb("rA", [128, 512], F32)
        rB = sb("rB", [128, 512], F32)
        O_ = sb("O", [128, 512], F32)
        sq = sb("sq", [128, 512], BF16)
        lnr = sb("lnr", [128, 512], F32)
        rt = sb("rt", [128, 512], F32)
        res = [sb(f"res{i}", [128, 512], BF16) for i in range(2)]
        pS = [[pst(f"pS{m}_{i}", [128, 512], F32) for i in range(2)] for m in range(2)]
        pO = [pst(f"pO{i}", [128, 512], F32) for i in range(2)]
        pZ = [pst(f"pZ{i}", [128, 512], F32) for i in range(2)]
        pss = pS[0][0]
        KSS = ('pS', 0, 0)
        MEMSET(ph, 'pool', epst[:], EPS, ['eps'])
        ph.dma('sp', lv[:], lamv_d, w=['lv'])
        ph.dma('sp', gs[:], bass.AP(gsub_row.tensor, gsub_row.offset, [[1, 128], [1, 1]]), w=['gs'])
        TT(ph, 'dve', lp[:], PV_(lv, 0, 1, 0, [[128, 2], [1, 64]]), PV_(lv, 0, 1, 64, [[128, 2], [1, 64]]), ALU.mult,
           ['lv'], ['lp'])
        ph.op('dve', lambda e: e.reduce_sum(out=ls[:], in_=lp[:], axis=AX.X), r=['lp'], w=['ls'])
        ACTF(ph, le[:], ls[:], AF.Exp, ['ls'], ['le'])
        TT(ph, 'dve', ll[:], le[:, 1:2], le[:, 0:1], ALU.subtract, ['le'], ['ll'])
        TS(ph, 'dve', ll[:], ll[:], -lam_init, None, ALU.add, None, ['ll'], ['ll'])
        MM(ph, pss[:, 0:1], onesf[0:1, :], ll[0:1, 0:1], True, True, ['ll'], [KSS])
        CP(ph, 'dve', nl[:], pss[:, 0:1], [], ['nl'], x=[KSS])
        TS(ph, 'dve', gs[:], gs[:], 1.0 - lam_init, None, ALU.mult, None, ['gs'], ['gs'])
        fv_t = fvn.tensor

        def loads(h):
            hs = h % 2
            for m in range(2):
                rq = ZR_FQ + m * 256 + h * 64
                rk = ZR_FK + m * 256 + h * 64
                ph.dma('sp', QT[hs][m * 64:(m + 1) * 64, :], zT[rq:rq + 64, :], w=[('QT', hs)])
                ph.dma('sp', KT[hs][m * 64:(m + 1) * 64, :], zT[rk:rk + 64, :], w=[('KT', hs)])
            for c0 in range(0, NT, 8):
                ph.dma('sp', Vf[hs][:, c0:c0 + 8, :],
                       bass.AP(fv_t, c0 * 128 * 512 + h * 128, [[512, 128], [128 * 512, 8], [1, 128]]), w=[('Vf', hs)])

        loads(0)
        fin, pend1, pend2 = [], [], []
        gq = 0
        for h in range(4):
            hs = h % 2
            if h < 3:
                loads(h + 1)
            for qt in range(NTT):
                zp = gq % 2
                gq += 1

                def pv(kc, hs=hs):
                    i4 = kc % 4
                    for m in range(2):
                        MM(ph, pO[m][:], Vf[hs][:, kc, :], PT[m][i4][:], kc == 0, kc == NT - 1,
                           [('PT', m, i4), ('Vf', hs)], [('pO', m)])
                        if kc % 2 == 1:
                            MM(ph, pZ[m][:], onesb[:], PT[m][i4][:], kc == 1, False, [('PT', m, i4)], [('pZ', m)])

                for kc in range(NT):
                    b = kc % 2
                    i4 = kc % 4
                    near = (4 * qt - 1 <= kc <= 4 * qt + 4)
                    for m in range(2):
                        MM(ph, pS[m][b][:], KT[hs][m * 64:(m + 1) * 64, kc * 128:(kc + 1) * 128],
                           QT[hs][m * 64:(m + 1) * 64, qt * 512:(qt + 1) * 512], True, not near,
                           [('KT', hs), ('QT', hs)], [('pS', m, b)])
                    if near:
                        for m in range(2):
                            col = m * 4 + h
                            for j in range(4):
                                off = kc - (4 * qt + j)
                                idx = off + 1 if abs(off) <= 1 else (3 if off < 0 else 4)
                                MM(ph, pS[m][b][:, j * 128:(j + 1) * 128], ident[:], fE[:, col * 5 + idx, :], False, j == 3,
                                   [], [('pS', m, b)])
                    for m in range(2):
                        col = m * 4 + h
                        if near:
                            ACTF(ph, PT[m][i4][:], pS[m][b][:], AF.Exp, [], [('PT', m, i4)], x=[('pS', m, b)], scale=0.125)
                        else:
                            side = 0 if kc < 4 * qt else 1
                            ACTF(ph, PT[m][i4][:], pS[m][b][:], AF.Exp, [], [('PT', m, i4)], x=[('pS', m, b)], scale=0.125,
                                 bias=farb[:, col * 2 + side:col * 2 + side + 1])
                        if kc % 2 == 0:
                            if kc == 0:
                                CP(ph, 'dve', zacc[m][zp][:], PT[m][i4][:], [('PT', m, i4)], [('za', m, zp)])
                            else:
                                TT(ph, 'dve', zacc[m][zp][:], zacc[m][zp][:], PT[m][i4][:], ALU.add,
                                   [('PT', m, i4), ('za', m, zp)], [('za', m, zp)])
                    if kc == 1 and fin:
                        fin.pop(0)()
                    if kc > 1:
                        pv(kc - 2)
                    if kc == 6 and pend1:
                        pend1.pop(0)()
                    if kc == 22 and pend2:
                        pend2.pop(0)()
                pv(NT - 2)
                pv(NT - 1)

                def finish(zp=zp):
                    for m in range(2):
                        MM(ph, pZ[m][:], onesf[:], zacc[m][zp][:], False, True, [('za', m, zp)], [('pZ', m)])
                    for m in range(2):
                        CP(ph, 'dve', Os[m][:], pO[m][:], [], [('Os', m)], x=[('pO', m)])
                    for m in range(2):
                        CP(ph, 'dve', Zs[m][:], pZ[m][:], [], [('Zs', m)], x=[('pZ', m)])

                def epi1():
                    RECIP(ph, rA[:], Zs[0][:], [('Zs', 0)], ['rA'])
                    RECIP(ph, rB[:], Zs[1][:], [('Zs', 1)], ['rB'])
                    TT(ph, 'dve', rA[:], Os[0][:], rA[:], ALU.mult, [('Os', 0), 'rA'], ['rA'])
                    TT(ph, 'dve', rB[:], Os[1][:], rB[:], ALU.mult, [('Os', 1), 'rB'], ['rB'])
                    STT(ph, 'dve', O_[:], rB[:], nl[:, 0:1], rA[:], ALU.mult, ALU.add, ['rA', 'rB', 'nl'], ['O'])
                    TT(ph, 'pool', sq[:], O_[:], O_[:], ALU.mult, ['O'], ['sq'])

                def epi2(h=h, qt=qt):
                    MM(ph, pss[:], onesb[:], sq[:], True, True, ['sq'], [KSS])
                    ACTF(ph, lnr[:], pss[:], AF.Ln, ['eps'], ['lnr'], x=[KSS], scale=1.0 / 128, bias=epst[:])
                    ACTF(ph, rt[:], lnr[:], AF.Exp, ['lnr'], ['rt'], scale=-0.5)
                    q2 = qt % 2
                    STT(ph, 'dve', res[q2][:], O_[:], gs[:, 0:1], rt[:], ALU.mult, ALU.mult, ['O', 'gs', 'rt'],
                        [('res', q2)])
                    ph.dma('pool', ocT[h * 128:(h + 1) * 128, qt * 512:(qt + 1) * 512], res[q2][:], r=[('res', q2)])

                fin.append(finish)
                pend1.append(epi1)
                pend2.append(epi2)
        for lst in (fin, pend1, pend2):
            for c in lst:
                c()
        ph.emit()


def precast_mlp_weights(ph, w_up_l, w_down_l, wub_d, wdb_d):
    for (src, dst) in ((w_up_l, wub_d), (w_down_l, wdb_d)):
        for q in range(4):
            ph.dma('pool', bass.AP(dst, q * 512 * 2048, [[2048, 512], [1, 2048]]),
                   bass.AP(src.tensor, src.offset + q * 512 * 2048, [[2048, 512], [1, 2048]]))


def norm_resid(ph, py, slot, xt, gpb, x_src_chunk, x_dst_chunk, tiles, epst, keyp):
    junk, ss, rs, tmp, xn = tiles['junk'][slot], tiles['ss'][slot], tiles['rs'][slot], tiles['tmp'][slot], tiles['xn'][slot]
    ph.dma('sp', xt[slot][:], x_src_chunk, w=[('xt', slot)])
    ACTF(ph, junk[:], py[:], AF.Square, [], [('junk', slot), ('ss', slot)], x=[keyp], accum_out=ss[:])
    ACTF(ph, rs[:], ss[:], AF.Sqrt, [('ss', slot), 'eps'], [('rs', slot)], scale=1.0 / D, bias=epst[:])
    RECIP(ph, rs[:], rs[:], [('rs', slot)], [('rs', slot)])
    STT(ph, 'dve', tmp[:], py[:], rs[:, 0:1], gpb[:], ALU.mult, ALU.mult, [('rs', slot), 'gpb'], [('tmp', slot)], x=[keyp])
    TT(ph, 'dve', xn[:], tmp[:], xt[slot][:], ALU.add, [('tmp', slot), ('xt', slot)], [('xn', slot)])
    ph.dma('pool', x_dst_chunk, xn[:], r=[('xn', slot)])


def phase_merge(nc, name, zT, oT3, w_branch_l, w_out_l, gpost_row, gmlp_row, x_src, x_dst, h2T_d, ident):
    ph = Phase(nc, name)
    with ExitStack() as st:
        sb = lambda n, shp, dt: st.enter_context(nc.sbuf_tensor(f"{name}_{n}", shp, dt))
        pst = lambda n, shp, dt: st.enter_context(nc.psum_tensor(f"{name}_{n}", shp, dt))
        wb = sb("wb", [128, 12, D], BF16)
        wo = sb("wo", [128, 8, D], BF16)
        gpb = sb("gpb", [128, D], F32)
        gmb = sb("gmb", [128, D], F32)
        epst = sb("eps", [128, 1], F32)
        oT = [sb(f"oT{b}", [128, 4, 512], BF16) for b in range(3)]
        gt = [sb(f"gt{i}", [128, 3, 512], BF16) for i in range(2)]
        tmpm = [[sb(f"tm{i}_{b}", [128, 512], F32) for b in range(3)] for i in range(2)]
        mT = [sb(f"mT{i}", [128, 8, 512], BF16) for i in range(2)]
        xt = [sb(f"xt{i}", [128, D], F32) for i in range(2)]
        tiles = {'junk': [sb(f"junk{i}", [128, D], BF16) for i in range(2)],
                 'ss': [sb(f"ss{i}", [128, 1], F32) for i in range(2)],
                 'rs': [sb(f"rs{i}", [128, 1], F32) for i in range(2)],
                 'tmp': [sb(f"tmp{i}", [128, D], F32) for i in range(2)],
                 'xn': [sb(f"xn{i}", [128, D], F32) for i in range(2)]}
        ss2 = [sb(f"ssb{i}", [128, 1], F32) for i in range(2)]
        rs2 = [sb(f"rsb{i}", [128, 1], F32) for i in range(2)]
        hb = [sb(f"hb{i}", [128, D], BF16) for i in range(2)]
        hst = [sb(f"hst{i}", [128, 8, 128], BF16) for i in range(2)]
        pM = [pst(f"pM{i}", [128, 512], F32) for i in range(2)]
        pY = [pst(f"pY{i}", [128, D], F32) for i in range(2)]
        psT = [pst(f"psT{i}", [128, 8, 128], BF16) for i in range(1)]
        MEMSET(ph, 'pool', epst[:], EPS, ['eps'])
        wbt = w_branch_l.tensor
        for b0 in range(0, 12, 4):
            ph.dma('pool', wb[:, b0:b0 + 4, :],
                   bass.AP(wbt, w_branch_l.offset + b0 * 128 * D, [[D, 128], [128 * D, 4], [1, D]]), w=['wb'])
        for b0 in range(0, 8, 4):
            ph.dma('pool', wo[:, b0:b0 + 4, :],
                   bass.AP(w_out_l.tensor, w_out_l.offset + b0 * 128 * D, [[D, 128], [128 * D, 4], [1, D]]), w=['wo'])
        ph.dma('sp', gpb[:], bcast_row(gpost_row), w=['gpb'])
        ph.dma('sp', gmb[:], bcast_row(gmlp_row), w=['gmb'])
        zt_t = zT.tensor
        mi = 0
        for tt in range(NTT):
            ms = tt % 2
            for b in range(3):
                ph.dma('sp', oT[b][:], bass.AP(oT3[b], tt * 512, [[S, 128], [128 * S, 4], [1, 512]]),
                       w=[('oT', b)])
            for mc in range(8):
                gsl = mc % 2
                ph.dma('sp', gt[gsl][:],
                       bass.AP(zt_t, (ZR_G + mc * 128) * S + tt * 512, [[S, 128], [8 * 128 * S, 3], [1, 512]]),
                       w=[('gt', gsl)])
                for b in range(3):
                    p = mi % 2
                    mi += 1
                    for kc in range(4):
                        MM(ph, pM[p][:], wb[:, b * 4 + kc, mc * 128:(mc + 1) * 128], oT[b][:, kc, :], kc == 0, kc == 3,
                           ['wb', ('oT', b)], [('pM', p)])
                    TT(ph, 'dve', tmpm[gsl][b][:], pM[p][:], gt[gsl][:, b, :], ALU.mult, [('gt', gsl)],
                       [('tm', gsl, b)], x=[('pM', p)])
                TT(ph, 'pool', tmpm[gsl][0][:], tmpm[gsl][0][:], tmpm[gsl][1][:], ALU.add,
                   [('tm', gsl, 0), ('tm', gsl, 1)], [('tm', gsl, 0)])
                TT(ph, 'pool', mT[ms][:, mc, :], tmpm[gsl][0][:], tmpm[gsl][2][:], ALU.add,
                   [('tm', gsl, 0), ('tm', gsl, 2)], [('mT', ms)])
            prev_tr = None

            def do_tr(tc, sl):
                for k in range(8):
                    TR(ph, psT[0][:, k, :], hb[sl][:, k * 128:(k + 1) * 128], ident[:], [('hb', sl)], ['psT'])
                CP(ph, 'act', hst[sl][:], psT[0][:], [], [('hst', sl)], x=['psT'])
                ph.dma('pool', h2T_d[:, :, tc * 128:(tc + 1) * 128], hst[sl][:], r=[('hst', sl)])

            for c4 in range(4):
                tc = tt * 4 + c4
                sl = tc % 2
                py = pY[sl]
                for half in range(2):
                    for kc in range(8):
                        MM(ph, py[:, half * 512:(half + 1) * 512], mT[ms][:, kc, c4 * 128:(c4 + 1) * 128],
                           wo[:, kc, half * 512:(half + 1) * 512], kc == 0, kc == 7, ['wo', ('mT', ms)], [('pY', sl)])
                if prev_tr is not None:
                    do_tr(*prev_tr)
                rows = slice(tc * 128, (tc + 1) * 128)
                norm_resid(ph, py, sl, xt, gpb, x_src[rows, :], x_dst[rows, :], tiles, epst, ('pY', sl))
                xn = tiles['xn'][sl]
                ACTF(ph, tiles['junk'][sl][:], xn[:], AF.Square, [('xn', sl)], [('junk', sl), ('ss2', sl)],
                     accum_out=ss2[sl][:])
                ACTF(ph, rs2[sl][:], ss2[sl][:], AF.Sqrt, [('ss2', sl), 'eps'], [('rs2', sl)], scale=1.0 / D, bias=epst[:])
                RECIP(ph, rs2[sl][:], rs2[sl][:], [('rs2', sl)], [('rs2', sl)])
                STT(ph, 'dve', hb[sl][:], xn[:], rs2[sl][:, 0:1], gmb[:], ALU.mult, ALU.mult,
                    [('xn', sl), ('rs2', sl), 'gmb'], [('hb', sl)])
                prev_tr = (tc, sl)
            do_tr(*prev_tr)
        ph.emit()


def phase_mlp(nc, name, w_up_l, w_down_l, wub_d, wdb_d, h2T_d, gpost_row, x_src, x_dst):
    DFF = 4 * D
    ph = Phase(nc, name)
    with ExitStack() as st:
        sb = lambda n, shp, dt: st.enter_context(nc.sbuf_tensor(f"{name}_{n}", shp, dt))
        pst = lambda n, shp, dt: st.enter_context(nc.psum_tensor(f"{name}_{n}", shp, dt))
        wd = sb("wd", [128, 32, D], BF16)
        aT = sb("aT", [128, 32, 512], BF16)
        h2 = [sb(f"h2{i}", [128, 8, 512], BF16) for i in range(2)]
        wu = [sb(f"wu{i}", [128, 8, 256], BF16) for i in range(2)]
        rl = [sb(f"rl{i}", [128, 512], F32) for i in range(2)]
        gpb = sb("gpb", [128, D], F32)
        epst = sb("eps", [128, 1], F32)
        xt = [sb(f"xt{i}", [128, D], F32) for i in range(2)]
        tiles = {'junk': [sb(f"junk{i}", [128, D], BF16) for i in range(2)],
                 'ss': [sb(f"ss{i}", [128, 1], F32) for i in range(2)],
                 'rs': [sb(f"rs{i}", [128, 1], F32) for i in range(2)],
                 'tmp': [sb(f"tmp{i}", [128, D], F32) for i in range(2)],
                 'xn': [sb(f"xn{i}", [128, D], F32) for i in range(2)]}
        pU = [pst(f"pU{i}", [128, 512], F32) for i in range(2)]
        pY = [pst(f"pY{i}", [128, D], F32) for i in range(2)]
        MEMSET(ph, 'pool', epst[:], EPS, ['eps'])
        ph.dma('sp', gpb[:], bcast_row(gpost_row), w=['gpb'])
        for c0 in range(0, 32, 8):
            ph.dma('sp', wd[:, c0:c0 + 8, :], bass.AP(wdb_d, c0 * 128 * D, [[D, 128], [128 * D, 8], [1, D]]),
                   w=['wd'])
        ui = 0
        for tt in range(NTT):
            hs = tt % 2
            ph.dma('sp', h2[hs][:], h2T_d[:, :, tt * 512:(tt + 1) * 512], w=[('h2', hs)])
            for fb in range(16):
                ws = fb % 2
                ph.dma('sp', wu[ws][:], bass.AP(wub_d, fb * 256, [[DFF, 128], [128 * DFF, 8], [1, 256]]),
                       w=[('wu', ws)])
                for fi in range(2):
                    fc = fb * 2 + fi
                    p = ui % 2
                    ui += 1
                    for kc in range(8):
                        MM(ph, pU[p][:], wu[ws][:, kc, fi * 128:(fi + 1) * 128], h2[hs][:, kc, :], kc == 0, kc == 7,
                           [('wu', ws), ('h2', hs)], [('pU', p)])
                    ACTF(ph, rl[p][:], pU[p][:], AF.Relu, [], [('rl', p)], x=[('pU', p)])
                    TT(ph, 'pool', aT[:, fc, :], rl[p][:], rl[p][:], ALU.mult, [('rl', p)], ['aT'])
            for c4 in range(4):
                tc = tt * 4 + c4
                sl = tc % 2
                py = pY[sl]
                for half in range(2):
                    for kc in range(32):
                        MM(ph, py[:, half * 512:(half + 1) * 512], aT[:, kc, c4 * 128:(c4 + 1) * 128],
                           wd[:, kc, half * 512:(half + 1) * 512], kc == 0, kc == 31, ['wd', 'aT'], [('pY', sl)])
                rows = slice(tc * 128, (tc + 1) * 128)
                norm_resid(ph, py, sl, xt, gpb, x_src[rows, :], x_dst[rows, :], tiles, epst, ('pY', sl))
        ph.emit()

def build(nlayers=DEPTH, stop_after=None, dbg=()):
    nc = bass.Bass("TRN2", target_bir_lowering=False)

    def din(name, shape, dt=F32):
        return nc.dram_tensor(name, list(shape), dt, kind="ExternalInput")

    def dscr(name, shape, dt):
        kind = "ExternalOutput" if name in dbg else "Internal"
        return nc.dram_tensor(name, list(shape), dt, kind=kind)

    x = din("x", [S, D])
    pos_d = din("pos", [128, NT], I32)
    invf_d = din("invf", [128, 16])
    dbias_d = din("dbias", [128, 36, 128])
    fbias_d = din("fbias", [128, 40, 128])
    farb_d = din("farb", [128, 16])
    lamv_d = din("lamv", [DEPTH, 4, 64])
    w_in = din("w_in", [DEPTH, D, IN_COLS])
    g_mix_pre = din("g_mix_pre", [DEPTH, D])
    g_q = din("g_q", [DEPTH, 256])
    g_kv = din("g_kv", [DEPTH, 128])
    w_uq = din("w_uq", [DEPTH, 256, 768])
    w_ukv = din("w_ukv", [DEPTH, 128, 1024])
    g_diff_sub = din("g_diff_sub", [DEPTH, 128])
    w_branch = din("w_branch", [DEPTH, 3 * 512, D])
    w_out = din("w_out", [DEPTH, D, D])
    g_mix_post = din("g_mix_post", [DEPTH, D])
    g_mlp_pre = din("g_mlp_pre", [DEPTH, D])
    w_up = din("w_up", [DEPTH, D, 4 * D])
    w_down = din("w_down", [DEPTH, 4 * D, D])
    g_mlp_post = din("g_mlp_post", [DEPTH, D])
    ident_d = din("ident", [128, 128], BF16)
    y = nc.dram_tensor("y", [S, D], F32, kind="ExternalOutput")

    zT = dscr("zT", [ZT_ROWS, S], BF16)
    lat = dscr("lat", [S, 416], F32)
    dvn = dscr("dvn", [S, 512], BF16)
    fvn = dscr("fvn", [S, 512], BF16)
    oaT = dscr("oaT", [512, S], BF16)
    obT = dscr("obT", [512, S], BF16)
    ocT = dscr("ocT", [512, S], BF16)
    xa = dscr("xa", [S, D], F32)
    xb = dscr("xb", [S, D], F32)
    h2T_d = dscr("h2T_d", [128, 8, S], BF16)
    wub_d = dscr("wub_d", [D, 4 * D], BF16)
    wdb_d = dscr("wdb_d", [4 * D, D], BF16)
    hT_d = dscr("hT_d", [128, 8, S], BF16)
    cs_d = dscr("cs_d", [128, 2, NT, 16], F32)

    with ExitStack() as top:
        sbt = lambda n, shp, dt: top.enter_context(nc.sbuf_tensor(n, shp, dt))
        ident = sbt("ident_sb", [128, 128], BF16)
        onesb = sbt("onesb", [128, 128], BF16)
        onesf = sbt("onesf", [128, 128], F32)
        cosT = sbt("cosT", [128, NT, 16], F32)
        sinT = sbt("sinT", [128, NT, 16], F32)
        dE = sbt("dE_sb", [128, 36, 128], F32)
        fE = sbt("fE_sb", [128, 40, 128], BF16)
        farb = sbt("farb_sb", [128, 16], F32)
        phase_init(nc, "init", ident, ident_d.ap(), pos_d.ap(), invf_d.ap(), cosT, sinT, onesb, onesf)
        phase_bias(nc, "bias", dbias_d.ap(), fbias_d.ap(), farb_d.ap(), dE, fE, farb)
        if "cs_d" in dbg:
            ph = Phase(nc, "dbgcs")
            ph.dma('sp', cs_d[:, 0, :, :], cosT[:])
            ph.dma('sp', cs_d[:, 1, :, :], sinT[:])
            ph.emit()
        done = False
        for l in range(nlayers):
            if stop_after == "I":
                break
            x_src = x.ap() if l == 0 else xb.ap()
            x_out = y.ap() if l == nlayers - 1 else xb.ap()
            with nc.sbuf_tensor(f"hT{l}", [128, 8, S], BF16) as hT:
                phase_win(nc, f"L{l}B", w_in[l], hT, zT, lat, dvn, fvn,
                          pre=lambda ph, st, hook: phase_norm_T(nc, f"L{l}A", x_src, g_mix_pre[l, :], hT, ident, ph=ph, st=st,
                                                               after_chunk=hook))
                if stop_after == "B":
                    break
            if stop_after not in ("E", "F") or True:
                with ExitStack() as stl:
                    cqnT = stl.enter_context(nc.sbuf_tensor(f"cqnT{l}", [128, 2, S], BF16))
                    ckvnT = stl.enter_context(nc.sbuf_tensor(f"ckvnT{l}", [128, S], BF16))
                    krT = stl.enter_context(nc.sbuf_tensor(f"krT{l}", [128, S], BF16))
                    phase_latprep(nc, f"L{l}C", lat.ap(), g_q[l, :], g_kv[l, :], cosT, sinT, cqnT, ckvnT, krT, ident)
                    if stop_after == "C":
                        break
                    if not SKIP_MLA:
                        phase_mla(nc, f"L{l}D", w_uq[l], w_ukv[l], cqnT, ckvnT, krT, cosT, sinT, oaT, ident, onesf,
                                  extra=lambda ph: precast_mlp_weights(ph, w_up[l], w_down[l], wub_d, wdb_d))
                if stop_after == "D":
                    break
            phase_dil(nc, f"L{l}E", zT.ap(), dvn.ap(), dE, obT, onesb, ident)
            if stop_after == "E":
                break
            phase_diff(nc, f"L{l}F", l, zT.ap(), fvn.ap(), fE, farb,
                       bass.AP(lamv_d, l * 256, [[256, 1], [64, 4], [1, 64]]), g_diff_sub[l, :], ocT, onesb, onesf, ident)
            if stop_after == "F":
                break
            phase_merge(nc, f"L{l}G", zT.ap(), (oaT, obT, ocT), w_branch[l], w_out[l], g_mix_post[l, :],
                        g_mlp_pre[l, :], x_src, xa.ap(), h2T_d, ident)
            if stop_after == "G":
                break
            phase_mlp(nc, f"L{l}H", w_up[l], w_down[l], wub_d, wdb_d, h2T_d, g_mlp_post[l, :], xa.ap(), x_out)
            if l == nlayers - 1:
                done = True
        if not done:
            ph = Phase(nc, "fin")
            ph.dma('sp', y.ap(), x.ap())
            ph.emit()
    return nc


SKIP_MLA = False


def _rel_bucket_np(rel):
    rel = np.asarray(rel, dtype=np.int64)
    nb, max_exact = 16, 8
    ret = np.where(rel > 0, nb, 0)
    n = np.abs(rel)
    try:
        import jax
        import jax.numpy as jnp
        with jax.default_device(jax.devices("cpu")[0]):
            nn = jnp.asarray(n.astype(np.int32))
            large = max_exact + (jnp.log(jnp.maximum(nn, 1).astype(jnp.float32) / max_exact)
                                 / math.log(128 / max_exact) * (nb - max_exact)).astype(jnp.int32)
            large = np.asarray(large).astype(np.int64)
    except Exception:
        lf = (np.log(np.maximum(n, 1).astype(np.float32) / np.float32(max_exact)).astype(np.float32)
              / np.float32(math.log(128 / max_exact))).astype(np.float32) * np.float32(nb - max_exact)
        large = max_exact + lf.astype(np.int32).astype(np.int64)
    large = np.minimum(large, nb - 1)
    return ret + np.where(n < max_exact, n, large)


def host_consts(rel_bias=None):
    invf = (10000.0 ** (-np.arange(16, dtype=np.float32) / 16)).astype(np.float32)
    out = {"ident": np.eye(128, dtype=np.float32).astype(ml_dtypes.bfloat16),
           "invf": np.ascontiguousarray(np.broadcast_to(invf[None, :], (128, 16)))}
    if rel_bias is None:
        rel_bias = np.zeros((32, 20), np.float32)
    rb = np.asarray(rel_bias, dtype=np.float32)
    MASK = np.float32(-3750.0)
    k = np.arange(128)[:, None]
    q = np.arange(128)[None, :]
    dbias = np.full((128, 36, 128), MASK, np.float32)
    for g in range(3):
        d = DIL[g]
        for h in range(4):
            col = g * 4 + h
            for ti, m in enumerate((k - 64 - q, k + 64 - q, k - q)):
                val = rb[_rel_bucket_np(m * d), col]
                dbias[:, col * 3 + ti, :] = np.where(np.abs(m) <= 64, val, MASK)
    fbias = np.zeros((128, 40, 128), np.float32)
    farb = np.zeros((128, 16), np.float32)
    for c8 in range(8):
        col = 12 + c8
        for idx, off in enumerate((-1, 0, 1)):
            fbias[:, c8 * 5 + idx, :] = rb[_rel_bucket_np(off * 128 + k - q), col]
        fbias[:, c8 * 5 + 3, :] = rb[15, col]
        fbias[:, c8 * 5 + 4, :] = rb[31, col]
        farb[:, c8 * 2 + 0] = rb[15, col]
        farb[:, c8 * 2 + 1] = rb[31, col]
    out.update({"dbias": dbias, "fbias": fbias, "farb": farb})
    return out


def core_inputs(inputs, c, consts):
    g = lambda k: np.asarray(inputs[k])
    m = {"x": np.ascontiguousarray(g("x")[c]),
         "pos": np.ascontiguousarray(g("positions")[c].astype(np.int32).reshape(NT, 128).T),
         "lamv": np.ascontiguousarray(np.stack([g("lam_q1"), g("lam_k1"), g("lam_q2"), g("lam_k2")], axis=1)),
         "w_in": g("w_in"), "g_mix_pre": g("g_mix_pre"),
         "g_q": g("g_q"), "g_kv": g("g_kv"),
         "w_uq": g("w_uq").reshape(DEPTH, 256, 768),
         "w_ukv": g("w_ukv").reshape(DEPTH, 128, 1024),
         "g_diff_sub": g("g_diff_sub"),
         "w_branch": g("w_branch").reshape(DEPTH, 3 * 512, D),
         "w_out": g("w_out"), "g_mix_post": g("g_mix_post"), "g_mlp_pre": g("g_mlp_pre"),
         "w_up": g("w_up"), "w_down": g("w_down"), "g_mlp_post": g("g_mlp_post")}
    m.update(consts)
    return m


def kernel(**inputs):
    nc = build()
    consts = host_consts(inputs["rel_bias"])
    in_maps = [core_inputs(inputs, c, consts) for c in range(NCORES)]
    res = run_bass_kernel_spmd(nc, in_maps, core_ids=list(range(NCORES)))
    return np.stack([np.asarray(r["y"]) for r in res.results], axis=0).astype(np.float32)
```
